# Optimizing a Trainium2 kernel written in Bass

```python
import jax, jax.numpy as jnp
from jax import lax
import numpy as np

D_MODEL = 1024
BATCH = 8
SEQ = 2048
DEPTH = 1
DEC_BATCH = 128
DEC_SEQ = 8
PAST_LEN = 8192
PAGE_SIZE = 128

f32 = jnp.float32
MIX_WIDTH = D_MODEL
GLA_WIDTH = MIX_WIDTH // 2
GLA_HEADS = 4
GLA_DV = GLA_WIDTH // GLA_HEADS
GLA_DK = GLA_DV // 2
GLA_GATE_RANK = 16
GLA_TAU = 16.0
GLA_CHUNK = 64
SWA_WIDTH = MIX_WIDTH - GLA_WIDTH
HEAD_DIM = 64
N_Q_HEADS = SWA_WIDTH // HEAD_DIM
N_KV_HEADS = 2
GQA_GROUP = N_Q_HEADS // N_KV_HEADS
WINDOW = 128
ROPE_THETA = 10000.0
D_FF = 4 * D_MODEL
EPS = 1e-6
IN_SPLITS = (GLA_HEADS * GLA_DK, GLA_HEADS * GLA_DK, GLA_WIDTH, GLA_WIDTH, GLA_GATE_RANK,
             N_Q_HEADS * HEAD_DIM, N_KV_HEADS * HEAD_DIM, N_KV_HEADS * HEAD_DIM)
IN_WIDTH = sum(IN_SPLITS)
SPLIT_POINTS = tuple(int(p) for p in np.cumsum(IN_SPLITS)[:-1])

kernel_name = "hymba_gla_swa_sink_adaln_decoder_step"


def rms_norm(x, w):
    xf = x.astype(f32)
    y = xf * lax.rsqrt(jnp.mean(xf * xf, axis=-1, keepdims=True) + EPS) * w.astype(f32)
    return y.astype(x.dtype)


def rope(x, pos):
    half = HEAD_DIM // 2
    inv = jnp.power(ROPE_THETA, -jnp.arange(half, dtype=f32) * 2.0 / HEAD_DIM)
    ang = pos.astype(f32)[:, None] * inv[None, :]
    cos = jnp.cos(ang)[:, None, :]
    sin = jnp.sin(ang)[:, None, :]
    xf = x.astype(f32)
    x1, x2 = xf[..., :half], xf[..., half:]
    return jnp.concatenate([x1 * cos - x2 * sin, x2 * cos + x1 * sin], axis=-1).astype(x.dtype)


def gla_recurrence(q, k, v, log_a, s0):
    b, l, h, _ = q.shape
    dv = v.shape[-1]
    c = min(GLA_CHUNK, l)
    n = -(-l // c)
    pad = n * c - l
    if pad:
        padw = ((0, 0), (0, pad), (0, 0), (0, 0))
        q, k, v, log_a = [jnp.pad(t, padw) for t in (q, k, v, log_a)]

    def to_chunks(t):
        return t.astype(f32).reshape(b, n, c, h, t.shape[-1]).transpose(1, 0, 3, 2, 4)

    qc, kc, vc, ac = to_chunks(q), to_chunks(k), to_chunks(v), to_chunks(log_a)
    causal = jnp.tril(jnp.ones((c, c), bool))[:, :, None]

    def step(s, inp):
        qi, ki, vi, ai = inp
        bcum = jnp.cumsum(ai, axis=-2)
        diff = bcum[..., :, None, :] - bcum[..., None, :, :]
        decay = jnp.exp(jnp.where(causal, diff, -jnp.inf))
        attn = jnp.einsum('bhtd,bhsd,bhtsd->bhts', qi, ki, decay)
        o = (jnp.einsum('bhts,bhsv->bhtv', attn, vi)
             + jnp.einsum('bhtd,bhdv->bhtv', qi * jnp.exp(bcum), s))
        blast = bcum[..., -1:, :]
        s_new = (jnp.exp(blast)[..., 0, :, None] * s
                 + jnp.einsum('bhsd,bhsv->bhdv', ki * jnp.exp(blast - bcum), vi))
        return s_new, o

    s_fin, oc = lax.scan(step, s0.astype(f32), (qc, kc, vc, ac))
    o = oc.transpose(1, 0, 3, 2, 4).reshape(b, n * c, h, dv)[:, :l]
    return o, s_fin


def sink_attention(q, k, v, mask, sinks):
    s = jnp.einsum('...qhgd,...khd->...hgqk', q.astype(f32), k.astype(f32)) * (HEAD_DIM ** -0.5)
    s = jnp.where(mask, s, -jnp.inf)
    sink = sinks.astype(f32).reshape(N_KV_HEADS, GQA_GROUP, 1, 1)
    m = jnp.maximum(jnp.max(s, axis=-1, keepdims=True), sink)
    p = jnp.exp(s - m)
    denom = jnp.sum(p, axis=-1, keepdims=True) + jnp.exp(sink - m)
    return jnp.einsum('...hgqk,...khd->...qhgd', p / denom, v.astype(f32))


def swa_banded(q, k, v, sinks):
    b, t = q.shape[:2]
    nb = t // WINDOW
    qb = q.reshape(b, nb, WINDOW, N_KV_HEADS, GQA_GROUP, HEAD_DIM)

    def band(x):
        xb = x.reshape(b, nb, WINDOW, N_KV_HEADS, HEAD_DIM)
        prev = jnp.pad(xb, ((0, 0), (1, 0), (0, 0), (0, 0), (0, 0)))[:, :-1]
        return jnp.concatenate([prev, xb], axis=2)

    blk = jnp.arange(nb)[:, None]
    qpos = blk * WINDOW + jnp.arange(WINDOW)[None, :]
    kpos = (blk - 1) * WINDOW + jnp.arange(2 * WINDOW)[None, :]
    d = qpos[:, :, None] - kpos[:, None, :]
    mask = (d >= 0) & (d < WINDOW) & (kpos[:, None, :] >= 0)
    mask = mask[:, None, None]
    o = sink_attention(qb, band(k), band(v), mask, sinks)
    return o.reshape(b, t, SWA_WIDTH)


def swa_with_buffer(q, k, v, past_k, past_v, sinks):
    b, t = q.shape[:2]
    n_past = past_k.shape[1]
    keys = jnp.concatenate([past_k.astype(k.dtype), k], axis=1)
    vals = jnp.concatenate([past_v.astype(v.dtype), v], axis=1)
    qpos = jnp.arange(t)
    kpos = jnp.arange(n_past + t) - n_past
    d = qpos[:, None] - kpos[None, :]
    mask = ((d >= 0) & (d < WINDOW))[None, None]
    o = sink_attention(q.reshape(b, t, N_KV_HEADS, GQA_GROUP, HEAD_DIM), keys, vals, mask, sinks)
    return o.reshape(b, t, SWA_WIDTH), keys[:, -WINDOW:], vals[:, -WINDOW:]


def decoder_layer(x, c, pos, gla_state, past_k, past_v,
                  w_ada, b_ada, norm1_w, norm2_w, w_in, w_gate_up, b_gate, gla_norm_w,
                  q_norm_w, k_norm_w, sinks, w_out, w_ff1, w_ff2):
    bsz, t, _ = x.shape
    mod = jnp.einsum('bd,de->be', jax.nn.silu(c), w_ada) + b_ada
    sh1, sc1, g1, sh2, sc2, g2 = jnp.split(mod[:, None, :], 6, axis=-1)

    hn = rms_norm(x, norm1_w) * (1 + sc1) + sh1
    proj = jnp.einsum('btd,de->bte', hn, w_in)
    gq, gk, gv, gr, glr, sq, sk, sv = jnp.split(proj, SPLIT_POINTS, axis=-1)

    gq = gq.reshape(bsz, t, GLA_HEADS, GLA_DK) * (GLA_DK ** -0.5)
    gk = gk.reshape(bsz, t, GLA_HEADS, GLA_DK)
    gv = gv.reshape(bsz, t, GLA_HEADS, GLA_DV)
    gate_logit = (jnp.einsum('btr,re->bte', glr, w_gate_up) + b_gate).astype(f32)
    log_a = (jax.nn.log_sigmoid(gate_logit) / GLA_TAU).reshape(bsz, t, GLA_HEADS, GLA_DK)
    if gla_state is None:
        gla_state = jnp.zeros((bsz, GLA_HEADS, GLA_DK, GLA_DV), f32)
    go, gla_new = gla_recurrence(gq, gk, gv, log_a, gla_state)
    go = rms_norm(go, gla_norm_w) * jax.nn.silu(gr.reshape(bsz, t, GLA_HEADS, GLA_DV).astype(f32))
    go = go.reshape(bsz, t, GLA_WIDTH).astype(x.dtype)

    sq = rope(rms_norm(sq.reshape(bsz, t, N_Q_HEADS, HEAD_DIM), q_norm_w), pos)
    sk = rope(rms_norm(sk.reshape(bsz, t, N_KV_HEADS, HEAD_DIM), k_norm_w), pos)
    sv = sv.reshape(bsz, t, N_KV_HEADS, HEAD_DIM)
    if past_k is None:
        so = swa_banded(sq, sk, sv, sinks)
        k_keep, v_keep = sk[:, -WINDOW:], sv[:, -WINDOW:]
    else:
        so, k_keep, v_keep = swa_with_buffer(sq, sk, sv, past_k, past_v, sinks)

    mixed = jnp.einsum('bte,ed->btd', jnp.concatenate([go, so.astype(x.dtype)], axis=-1), w_out)
    h = x + g1 * mixed

    hn2 = rms_norm(h, norm2_w) * (1 + sc2) + sh2
    ff = jnp.einsum('btf,fd->btd', jnp.square(jax.nn.relu(jnp.einsum('btd,df->btf', hn2, w_ff1))), w_ff2)
    y = h + g2 * ff
    return y, gla_new, k_keep, v_keep


def setup_inputs(seed: int = 0) -> dict:
    key = jax.random.key(seed)
    ks = jax.random.split(key, 24)
    nrm = jax.random.normal
    return {
        "x_prompt": nrm(ks[0], (BATCH, SEQ, D_MODEL), f32),
        "x_sample": nrm(ks[1], (DEC_BATCH, DEC_SEQ, D_MODEL), f32),
        "state_gla": nrm(ks[2], (DEPTH, DEC_BATCH, GLA_HEADS, GLA_DK, GLA_DV), f32),
        "cache_swa_k": nrm(ks[3], (DEPTH, DEC_BATCH, WINDOW, N_KV_HEADS, HEAD_DIM), f32),
        "cache_swa_v": nrm(ks[4], (DEPTH, DEC_BATCH, WINDOW, N_KV_HEADS, HEAD_DIM), f32),
        "c_prompt": nrm(ks[5], (BATCH, D_MODEL), f32),
        "c_sample": nrm(ks[6], (DEC_BATCH, D_MODEL), f32),
        "w_ada": nrm(ks[7], (DEPTH, D_MODEL, 6 * D_MODEL), f32) * (0.5 * D_MODEL ** -0.5),
        "b_ada": nrm(ks[8], (DEPTH, 6 * D_MODEL), f32) * 0.01,
        "norm1_w": 1.0 + 0.1 * nrm(ks[9], (DEPTH, D_MODEL), f32),
        "norm2_w": 1.0 + 0.1 * nrm(ks[10], (DEPTH, D_MODEL), f32),
        "w_in": nrm(ks[11], (DEPTH, D_MODEL, IN_WIDTH), f32) * (D_MODEL ** -0.5),
        "w_gate_up": nrm(ks[12], (DEPTH, GLA_GATE_RANK, GLA_HEADS * GLA_DK), f32) * (GLA_GATE_RANK ** -0.5),
        "b_gate": nrm(ks[13], (DEPTH, GLA_HEADS * GLA_DK), f32) * 0.1,
        "gla_norm_w": 1.0 + 0.1 * nrm(ks[14], (DEPTH, GLA_DV), f32),
        "q_norm_w": 1.0 + 0.1 * nrm(ks[15], (DEPTH, HEAD_DIM), f32),
        "k_norm_w": 1.0 + 0.1 * nrm(ks[16], (DEPTH, HEAD_DIM), f32),
        "sinks": nrm(ks[17], (DEPTH, N_Q_HEADS), f32) * 0.5,
        "w_out": nrm(ks[18], (DEPTH, MIX_WIDTH, D_MODEL), f32) * (MIX_WIDTH ** -0.5),
        "w_ff1": nrm(ks[19], (DEPTH, D_MODEL, D_FF), f32) * (D_MODEL ** -0.5),
        "w_ff2": nrm(ks[20], (DEPTH, D_FF, D_MODEL), f32) * (D_FF ** -0.5),
    }


def reference(x_prompt, x_sample, state_gla, cache_swa_k, cache_swa_v, c_prompt, c_sample,
              w_ada, b_ada, norm1_w, norm2_w, w_in, w_gate_up, b_gate, gla_norm_w,
              q_norm_w, k_norm_w, sinks, w_out, w_ff1, w_ff2):
    pos_prompt = jnp.arange(x_prompt.shape[1])
    pos_sample = PAST_LEN + jnp.arange(x_sample.shape[1])
    yp, ys = x_prompt, x_sample
    gp_l, kp_l, vp_l, gs_l, ks_l, vs_l = [], [], [], [], [], []
    for l in range(DEPTH):
        lw = (w_ada[l], b_ada[l], norm1_w[l], norm2_w[l], w_in[l], w_gate_up[l], b_gate[l],
              gla_norm_w[l], q_norm_w[l], k_norm_w[l], sinks[l], w_out[l], w_ff1[l], w_ff2[l])
        yp, gp, kp, vp = decoder_layer(yp, c_prompt, pos_prompt, None, None, None, *lw)
        ys, gs, kq, vq = decoder_layer(ys, c_sample, pos_sample, state_gla[l],
                                       cache_swa_k[l], cache_swa_v[l], *lw)
        gp_l.append(gp); kp_l.append(kp); vp_l.append(vp)
        gs_l.append(gs); ks_l.append(kq); vs_l.append(vq)
    return (yp, ys,
            jnp.stack(gp_l), jnp.stack(kp_l), jnp.stack(vp_l),
            jnp.stack(gs_l), jnp.stack(ks_l), jnp.stack(vs_l))
```

```python
import contextlib
import math
import os

import numpy as np
import ml_dtypes

import concourse.bass as bass
import concourse.mybir as mybir
from concourse.bass_utils import run_bass_kernel_spmd

F32 = mybir.dt.float32
BF16 = mybir.dt.bfloat16
AF = mybir.ActivationFunctionType
ALU = mybir.AluOpType
AX = mybir.AxisListType

SAME_ENGINE_SYNC = os.environ.get("KSES", "1") == "1"
EPS = 1e-6
NT = 17
ST_TILES = [[0, 1, 2, 3], [4, 5, 6, 7], [8, 9, 10, 11], [12, 13, 14, 15, 16]]


class Prog:
    def __init__(self):
        self.ops = []
        self.last_w = {}
        self.readers = {}
        self.fence_id = None

    def fence(self, eng, fn):
        deps = set()
        last_eng = {}
        last_dsem = {}
        for j, o in enumerate(self.ops):
            if o["dma"]:
                last_dsem[o["dsem"]] = j
            last_eng[o["eng"]] = j
        deps.update(last_eng.values()); deps.update(last_dsem.values())
        if self.fence_id is not None:
            deps.add(self.fence_id)
        i = len(self.ops)
        self.ops.append(dict(eng=eng, fn=fn, deps=deps, dma=False, dsem=None, signal=False, fence=True))
        self.fence_id = i
        return i

    def add(self, eng, fn, reads=(), writes=(), dma=False, dsem=None, cost=None):
        if cost is None:
            cost = getattr(self, "next_cost", None)
        self.next_cost = None
        if cost is None:
            cost = {"pe": 0.25, "act": 0.5, "dve": 0.5, "pool": 1.0, "sp": 2.0}[eng] if not dma else 2.5
        if getattr(self, "sink", None) is not None:
            item = (eng, fn, tuple(reads), tuple(writes), dma, dsem, cost)
            if getattr(self, "group", None) is not None:
                self.group.append(item)
            else:
                self.sink.append([item])
            return -1
        i = len(self.ops)
        deps = set()
        if self.fence_id is not None:
            deps.add(self.fence_id)
        for k in reads:
            if k in self.last_w:
                deps.add(self.last_w[k])
        for k in writes:
            if k in self.last_w:
                deps.add(self.last_w[k])
            for r in self.readers.get(k, ()):
                deps.add(r)
        deps.discard(i)
        for k in writes:
            self.last_w[k] = i
            self.readers[k] = []
        for k in reads:
            if k not in writes:
                self.readers.setdefault(k, []).append(i)
        if dma and dsem is None:
            dsem = "d_" + str(writes[0])
        self.ops.append(dict(eng=eng, fn=fn, deps=deps, dma=dma, dsem=dsem, signal=False, cost=cost))
        return i

    def reschedule(self):
        ops = self.ops
        n = len(ops)
        order = []
        seg_start = 0
        bounds = [k for k, o in enumerate(ops) if o.get("fence")] + [n]
        prev = 0
        segs = []
        for b in bounds:
            if b > prev:
                segs.append((prev, b))
            if b < n:
                segs.append((b, b + 1))
            prev = b + 1
        finish = [0.0] * n
        eng_free = {}
        for (a, b) in segs:
            if b - a == 1:
                order.append(a)
                finish[a] = max([eng_free.get(ops[a]["eng"], 0.0)] + [finish[d] for d in ops[a]["deps"]]) + 0.1
                continue
            idx = list(range(a, b))
            indeg = {k: 0 for k in idx}
            succ = {k: [] for k in idx}
            for k in idx:
                for d in ops[k]["deps"]:
                    if a <= d < b:
                        indeg[k] += 1
                        succ[d].append(k)
            import heapq
            ready = [k for k in idx if indeg[k] == 0]
            t_eng = dict(eng_free)
            done = 0
            ready_set = set(ready)
            while ready_set:
                best = None
                for k in ready_set:
                    o = ops[k]
                    st = max([t_eng.get(o["eng"], 0.0)] + [finish[d] for d in o["deps"]])
                    key = (st, k)
                    if best is None or key < best[0]:
                        best = (key, k, st)
                _, k, st = best
                ready_set.discard(k)
                o = ops[k]
                if o["dma"]:
                    t_eng[o["eng"]] = st + 0.1
                    finish[k] = st + o["cost"]
                else:
                    t_eng[o["eng"]] = st + o["cost"]
                    finish[k] = st + o["cost"] + 0.15
                order.append(k)
                for s_ in succ[k]:
                    indeg[s_] -= 1
                    if indeg[s_] == 0:
                        ready_set.add(s_)
            eng_free = t_eng
        assert len(order) == n and len(set(order)) == n
        pos = {old: new for new, old in enumerate(order)}
        new_ops = [ops[k] for k in order]
        for o in new_ops:
            o["deps"] = set(pos[d] for d in o["deps"])
            for d in o["deps"]:
                pass
        for new, o in enumerate(new_ops):
            assert all(d < new for d in o["deps"]), "reschedule broke topological order"
        self.ops = new_ops
        self.est_time = max(finish) if finish else 0.0

    def capture(self, f, *a):
        self.sink = []
        self.group = None
        f(*a)
        out, self.sink = self.sink, None
        return out

    def begin_atomic(self):
        if getattr(self, "sink", None) is not None:
            self.group = []

    def end_atomic(self):
        if getattr(self, "sink", None) is not None and self.group is not None:
            self.sink.append(self.group)
            self.group = None

    def flush_positions(self, streams):
        items = []
        for si, (groups, lo, hi) in enumerate(streams):
            n = len(groups)
            for k, g in enumerate(groups):
                items.append((lo + (hi - lo) * (k + 0.5) / max(n, 1), si, k, g))
        items.sort(key=lambda t: (t[0], t[1], t[2]))
        for _, _, _, g in items:
            for it in g:
                self.add(*it[:6], cost=it[6])

    def flush_merged(self, A, B):
        ia = ib = 0
        na, nb = len(A), len(B)
        while ia < na or ib < nb:
            if ib >= nb or (ia < na and ia * nb <= ib * na):
                for it in A[ia]:
                    self.add(*it[:6], cost=it[6])
                ia += 1
            else:
                for it in B[ib]:
                    self.add(*it[:6], cost=it[6])
                ib += 1

    def emit(self, nc):
        if os.environ.get("KSCHED", "1") == "1":
            self.reschedule()
        ops = self.ops
        for o in ops:
            nd = set()
            for d in o["deps"]:
                p = ops[d]
                if (not p["dma"]) and p["eng"] == o["eng"] and not o.get("fence") and not p.get("fence"):
                    if o["eng"] == "pe" and not o["dma"]:
                        continue
                    if not SAME_ENGINE_SYNC:
                        continue
                nd.add(d)
            latest = {}
            nd2 = set()
            for d in nd:
                p = ops[d]
                if p["dma"]:
                    nd2.add(d)
                elif latest.get(p["eng"], -1) < d:
                    latest[p["eng"]] = d
            nd2.update(latest.values())
            o["deps"] = nd2
            for d in nd2:
                ops[d]["signal"] = True
        cnt = {}
        for o in ops:
            if o["dma"]:
                s = o["dsem"]
            elif o["signal"]:
                s = "e_" + o["eng"]
            else:
                o["tok"] = None
                continue
            cnt[s] = cnt.get(s, 0) + (16 if o["dma"] else 1)
            o["tok"] = (s, cnt[s])
        self.sem_counts = cnt
        with contextlib.ExitStack() as st:
            sems = {n: st.enter_context(nc.semaphore(n)) for n in sorted(cnt)}
            block = st.enter_context(nc.Block())
            final_tokens = {}
            for o in ops:
                if o["dma"]:
                    final_tokens[o["tok"][0]] = o["tok"][1]

            def run_engine(ename, eng):
                known = {}
                for o in ops:
                    if o["eng"] != ename:
                        continue
                    need = {}
                    for d in o["deps"]:
                        s, v = ops[d]["tok"]
                        if need.get(s, 0) < v:
                            need[s] = v
                    for s, v in need.items():
                        if known.get(s, 0) >= v:
                            continue
                        eng.wait_ge(sems[s], v)
                        known[s] = v
                    ins = o["fn"](eng)
                    if o["tok"] is not None:
                        ins.then_inc(sems[o["tok"][0]], 16 if o["dma"] else 1)
                if ename == "sp":
                    for s, v in final_tokens.items():
                        if known.get(s, 0) < v:
                            eng.wait_ge(sems[s], v)

            block.sync(lambda e: run_engine("sp", e))
            block.scalar(lambda e: run_engine("act", e))
            block.vector(lambda e: run_engine("dve", e))
            block.gpsimd(lambda e: run_engine("pool", e))
            block.tensor(lambda e: run_engine("pe", e))


CF_IDENT, CF_TRIP, CF_TRIS, CF_NEGP, CF_NEGS, CF_ROWSEL, CF_ROPE = 0, 128, 256, 384, 385, 401, 417
CF_W = 417 + NT * 128
CB_IDENT, CB_CMP, CB_CMS, CB_MASKB, CB_CMC, CB_ONES = 0, 128, 256, 384, 384 + 1536, 384 + 1536 + 8
CB_W = CB_ONES + 128


def _consts():
    s = np.arange(128)[:, None]
    t = np.arange(128)[None, :]
    same = (s // 8) == (t // 8)
    cf = np.zeros((128, CF_W), np.float32)
    cf[:, CF_IDENT:CF_IDENT + 128] = np.eye(128)
    cf[:, CF_TRIP:CF_TRIP + 128] = np.where(s <= t, -1.0 / 16, 0.0)
    cf[:, CF_TRIS:CF_TRIS + 128] = np.where((s <= t) & same, -1.0 / 16, 0.0)
    cf[:, CF_NEGP] = -1.0 / 16
    bsel = (np.arange(128)[:, None] // 8) == np.arange(16)[None, :]
    cf[:, CF_NEGS:CF_NEGS + 16] = np.where(bsel, -1.0 / 16, 0.0)
    cf[:, CF_ROWSEL:CF_ROWSEL + 16] = bsel.astype(np.float32)
    half = 32
    inv = np.power(np.float32(10000.0), -np.arange(half, dtype=np.float32) * np.float32(2.0) / np.float32(64)).astype(np.float32)
    for i in range(NT):
        if i < 16:
            pos = (i * 128 + np.arange(128)).astype(np.float32)
        else:
            pos = (8192 + (np.arange(128) % 8)).astype(np.float32)
        ang = (pos[:, None] * inv[None, :]).astype(np.float32)
        c = np.cos(ang).astype(np.float32)
        sn = np.sin(ang).astype(np.float32)
        base = CF_ROPE + i * 128
        cf[:, base:base + 32] = c
        cf[:, base + 32:base + 64] = c
        cf[:, base + 64:base + 96] = -sn
        cf[:, base + 96:base + 128] = sn
    cb = np.zeros((128, CB_W), np.float32)
    cb[:, CB_IDENT:CB_IDENT + 128] = np.eye(128)
    cb[:, CB_CMP:CB_CMP + 128] = (s <= t)
    cb[:, CB_CMS:CB_CMS + 128] = ((s <= t) & same)
    NEG = -30000.0
    m_prev = np.where(s > t, 0.0, NEG)
    m_cur = np.where(s <= t, 0.0, NEG)
    m_curs = np.where((s <= t) & same, 0.0, NEG)
    cb[:, CB_MASKB:CB_MASKB + 512] = np.tile(m_prev, (1, 4))
    cb[:, CB_MASKB + 512:CB_MASKB + 1024] = np.tile(m_cur, (1, 4))
    cb[:, CB_MASKB + 1024:CB_MASKB + 1536] = np.tile(m_curs, (1, 4))
    cb[:, CB_CMC:CB_CMC + 8] = (np.arange(128)[:, None] > np.arange(8)[None, :])
    cb[:, CB_ONES:CB_ONES + 128] = 1.0
    return cf, cb.astype(ml_dtypes.bfloat16)


def build():
    nc = bass.Bass("TRN2", target_bir_lowering=False)
    P = Prog()

    def din(name, shape, dt=F32):
        return nc.dram_tensor(name, shape, dt, kind="ExternalInput").ap()

    def dout(name, shape):
        return nc.dram_tensor(name, shape, F32, kind="ExternalOutput").ap()

    xp = din("xp", [2048, 1024]); xs = din("xs", [128, 1024]); cvec = din("cvec", [17, 1024])
    st_in = din("st_in", [16, 4, 64, 128]); ck_in = din("ck_in", [16, 128, 128]); cv_in = din("cv_in", [16, 128, 128])
    w_ada = din("w_ada", [1024, 6144]); b_ada = din("b_ada", [1, 6144])
    n1w = din("n1w", [1024]); n2w = din("n2w", [1024])
    w_in = din("w_in", [1024, 2320]); wgu = din("wgu", [16, 256]); b_gate = din("b_gate", [1, 256])
    gnw = din("gnw", [128]); qnw = din("qnw", [64]); knw = din("knw", [64]); sinks = din("sinks", [8])
    w_out = din("w_out", [1024, 1024]); w1 = din("w1", [1024, 4096]); w2 = din("w2", [4096, 1024])
    cst_f = din("cst_f", [128, CF_W]); cst_b = din("cst_b", [128, CB_W], BF16)

    yp = dout("yp", [2048, 1024]); ys = dout("ys", [128, 1024])
    gp = dout("gp", [4, 64, 128]); kp = dout("kp", [128, 128]); vp = dout("vp", [128, 128])
    gs = dout("gs", [16, 4, 64, 128]); ks = dout("ks", [16, 128, 128]); vs = dout("vs", [16, 128, 128])

    w1s = nc.dram_tensor("w1s", [1024, 4096], BF16, kind="Internal").ap()
    w2s = nc.dram_tensor("w2s", [4096, 1024], BF16, kind="Internal").ap()

    ARENA_W = 53200
    arena = nc.alloc_sbuf_tensor("arena", [128, ARENA_W], F32)
    ps = nc.alloc_psum_tensor("ps", [128, 8, 512], F32)
    psf = ps.rearrange("p b n -> p (b n)")

    def pbf(bank):
        return ps[:, bank, :].bitcast(BF16)

    class Carver:
        def __init__(self, lo, hi):
            self.off = lo; self.hi = hi

        def get(self, shape, dt):
            n = int(np.prod(shape[1:]))
            w = n if dt == F32 else (n + 1) // 2
            w = (w + 7) // 8 * 8
            assert self.off + w <= self.hi, ("SBUF carve overflow", shape, self.off, w, self.hi)
            v = arena[:, self.off:self.off + w]
            self.off += w
            if dt != F32:
                v = v.bitcast(dt)
            v = v[0:shape[0], 0:n]
            if len(shape) > 2:
                names = " ".join("a%d" % i for i in range(len(shape) - 1))
                v = v.rearrange("p (%s) -> p %s" % (names, names), **{"a%d" % i: shape[i + 1] for i in range(len(shape) - 1)})
            return v

    cp = Carver(0, ARENA_W)
    h = cp.get([128, NT, 1024], F32)
    g2P = cp.get([128, 1024], F32); g2S = cp.get([128, 1024], F32)
    a1 = cp.get([128, 8, 17], F32); sh1 = cp.get([128, 8, 17], F32)
    a2 = cp.get([128, 8, 17], F32); sh2 = cp.get([128, 8, 17], F32)
    cf = cp.get([128, CF_ROPE], F32)
    cb = cp.get([128, CB_W], BF16)
    ssa = cp.get([128, 64], F32)
    ssb = cp.get([128, 64], F32)
    smallf = cp.get([128, 256], F32)
    n1f = cp.get([128, 8], F32); n2f = cp.get([128, 8], F32)
    ident = cb[:, CB_IDENT:CB_IDENT + 128]
    ident32 = cf[:, CF_IDENT:CF_IDENT + 128]
    ones_row = cb[0:1, CB_ONES:CB_ONES + 128]
    X0 = cp.off

    ca = Carver(X0, ARENA_W)
    win = ca.get([128, 8, 2560], BF16)
    wout = ca.get([128, 8, 1024], BF16)
    g1P = ca.get([128, 1024], F32); g1S = ca.get([128, 1024], F32)
    ropeR = [ca.get([128, 128], F32) for _ in range(2)]
    wqk = ca.get([128, 640], F32); gnwb = ca.get([128, 128], F32)
    esink = ca.get([128, 8], F32); negc = ca.get([128, 8], F32)
    bgrow = ca.get([1, 256], BF16)
    S32 = ca.get([128, 2, 128], F32); Sbf = ca.get([128, 2, 128], BF16); tmpS = ca.get([128, 2, 128], F32)
    kTr = [ca.get([128, 128], BF16) for _ in range(2)]
    WA0 = ca.off
    tbuf = ca.get([128, 1024], BF16)
    mixR = [ca.get([128, 1024], BF16) for _ in range(2)]
    rotbR = [ca.get([128, 640], BF16) for _ in range(2)]
    vaug = [ca.get([128, 2, 65], BF16) for _ in range(3)]
    hnT = ca.get([128, 8, 128], BF16)
    e1 = ca.get([128, 256], F32)
    spb = ca.get([128, 256], F32)
    enb = ca.get([128, 256], F32); eblR = [ca.get([128, 2, 16], F32) for _ in range(2)]
    qkt = ca.get([128, 512], BF16); vb = ca.get([128, 512], BF16)
    qkT = ca.get([128, 4, 128], BF16)
    attmb = ca.get([128, 512], BF16)
    gr2 = ca.get([128, 640], F32)
    gateb = ca.get([128, 512], BF16)
    qkn = ca.get([128, 640], F32); r1 = ca.get([128, 640], F32)
    sv32 = ca.get([128, 128], F32)
    osqb = ca.get([128, 512], BF16)
    qTb = ca.get([128, 4, 128], BF16)
    E2 = [ca.get([128, 512], BF16) for _ in range(2)]
    mixT = ca.get([128, 8, 128], BF16)
    tmpf = ca.get([128, 512], F32)
    WA1 = ca.off
    Sst = ca.get([128, 2, 2, 128], F32)
    Sstb = ca.get([128, 2, 2, 128], BF16)
    Sout = ca.get([128, 2, 2, 128], F32)
    oTb = ca.get([128, 4, 128], BF16)
    km = ca.get([128, 256], BF16)
    ckb = ca.get([128, 8, 128], BF16); kcT = ca.get([128, 8, 128], BF16)
    vca = ca.get([128, 8, 2, 65], BF16)
    oTc = ca.get([128, 2, 4, 128], BF16)
    A_END = ca.off

    cpz = Carver(WA0, ARENA_W)
    NWA = 3
    wada = [cpz.get([128, 8, 512], BF16) for _ in range(NWA)]
    csb = cpz.get([17, 1024], F32); csb2 = cpz.get([17, 1024], F32); scb = cpz.get([17, 1024], BF16)
    scT = cpz.get([128, 8, 17], BF16); scP = cpz.get([128, 8, 128], BF16); scS = cpz.get([128, 8, 128], BF16)
    bfm = cpz.get([128, 48], F32)
    modraw = [cpz.get([128, 8, 17], F32) for _ in range(2)]
    glr = cpz.get([128, 8, 16], F32); glrT = cpz.get([16, 8, 128], F32); wgus = cpz.get([16, 256], F32)
    wtmp = cpz.get([128, 64], F32)
    assert cpz.off <= ARENA_W

    cbz = Carver(X0, ARENA_W)
    t2 = cbz.get([128, 1024], BF16)
    hn2T_a = cbz.get([128, 8, 512], BF16)
    W1R = [cbz.get([128, 8, 512], BF16) for _ in range(3)]
    assert cbz.off <= X0 + 10240
    hn2T_b = cbz.get([128, 8, 640], BF16)
    hn2TR = [hn2T_a, hn2T_b]
    hT = cbz.get([128, 32, 640], BF16)
    W2R = [cbz.get([128, 4, 512], BF16) for _ in range(4)]
    rl = [cbz.get([128, 640], BF16) for _ in range(2)]
    ytmpR = [cbz.get([128, 512], F32) for _ in range(5)]
    assert cbz.off <= ARENA_W

    def dma(q, out, in_, reads, writes, dsem=None, nonc=False):
        nb = 1
        for d in in_.shape:
            nb *= d
        nb *= 4 if in_.dtype == F32 else 2
        P.next_cost = 2.0 + nb / 3.0e5
        if nonc:
            def f(e):
                with nc.allow_non_contiguous_dma(reason="small strided load"):
                    return e.dma_start(out=out, in_=in_)
        else:
            def f(e):
                return e.dma_start(out=out, in_=in_)
        P.add(q, f, reads=reads, writes=writes, dma=True, dsem=dsem)

    def nfree(ap):
        n = 1
        for d in ap.shape[1:]:
            n *= d
        return n

    def act(out, in_, func, reads, writes, bias=None, scale=None, accum=None):
        P.next_cost = 0.28 + nfree(out) / 1200.0
        kw = {}
        if bias is not None: kw["bias"] = bias
        if scale is not None: kw["scale"] = scale
        if accum is not None: kw["accum_out"] = accum
        P.add("act", lambda e: e.activation(out=out, in_=in_, func=func, **kw), reads=reads, writes=writes)

    def tt(out, in0, in1, op, reads, writes, eng="dve"):
        P.next_cost = (0.1 + nfree(out) / 900.0) if eng == "dve" else (0.2 + nfree(out) / 420.0)
        P.add(eng, lambda e: e.tensor_tensor(out=out, in0=in0, in1=in1, op=op), reads=reads, writes=writes)

    def ts(out, in0, s1, op0, reads, writes, s2=None, op1=None, eng="dve"):
        P.next_cost = 0.1 + nfree(out) / 1400.0
        if op1 is None:
            P.add(eng, lambda e: e.tensor_scalar(out=out, in0=in0, scalar1=s1, scalar2=None, op0=op0), reads=reads, writes=writes)
        else:
            P.add(eng, lambda e: e.tensor_scalar(out=out, in0=in0, scalar1=s1, scalar2=s2, op0=op0, op1=op1), reads=reads, writes=writes)

    def cpy(eng, out, in_, reads, writes):
        P.next_cost = (0.28 + nfree(out) / 1200.0) if eng == "act" else (0.1 + nfree(out) / 1000.0)
        if eng == "act":
            P.add("act", lambda e: e.activation(out=out, in_=in_, func=AF.Copy), reads=reads, writes=writes)
        else:
            P.add(eng, lambda e: e.tensor_copy(out=out, in_=in_), reads=reads, writes=writes)

    def mm(out, lhsT, rhs, start, stop, reads, writes):
        P.next_cost = 0.03 + max(nfree(out), 64) / 1800.0 * (4.0 if lhsT.dtype == F32 else 1.0)
        P.add("pe", lambda e: e.matmul(out, lhsT=lhsT, rhs=rhs, start=start, stop=stop), reads=reads, writes=writes)

    def tr(out, in_, idn, reads, writes):
        P.next_cost = 0.11
        P.add("pe", lambda e: e.transpose(out=out, in_=in_, identity=idn), reads=reads, writes=writes)

    DBG = {}

    def dbg(name, ap, reads):
        if not os.environ.get("KDBG"):
            return
        shp = list(ap.shape)
        d = nc.dram_tensor("dbg_" + name, shp, ap.dtype, kind="ExternalOutput").ap()
        DBG[name] = d
        dma("sp", d, ap, reads, ["dbg_" + name])

    def rstd_from_ss(ss_ap, n, out_ap, tmp_ap, D, keys_r, key_w):
        act(tmp_ap, ss_ap, AF.Ln, reads=keys_r, writes=[key_w + "_ln"], bias=EPS, scale=1.0 / D)
        act(out_ap, tmp_ap, AF.Exp, reads=[key_w + "_ln"], writes=[key_w], scale=-0.5)

    dma("sp", cf, cst_f[:, 0:CF_ROPE], [], ["cf"])
    dma("sp", cb, cst_b, [], ["cb"])
    dma("sp", csb, cvec, [], ["csb"])
    dma("sp", n1f, n1w.rearrange("(k p) -> p k", p=128), [], ["n1f"], nonc=True)
    dma("sp", n2f, n2w.rearrange("(k p) -> p k", p=128), [], ["n2f"], nonc=True)
    dma("sp", bfm, b_ada.rearrange("o (c p) -> p (o c)", p=128), [], ["bfm"], nonc=True)
    dma("sp", glr, w_in[:, 1536:1552].rearrange("(k p) r -> p k r", p=128), [], ["glr"], nonc=True)
    dma("sp", wgus, wgu, [], ["wgus"])
    dma("sp", wqk[:, 0:512].rearrange("p (h d) -> p h d", h=8), qnw.partition_broadcast(128).rearrange("p (o d) -> p o d", o=1).broadcast_to([128, 8, 64]), [], ["wqk_q"], nonc=True)
    dma("sp", wqk[:, 512:640].rearrange("p (h d) -> p h d", h=2), knw.partition_broadcast(128).rearrange("p (o d) -> p o d", o=1).broadcast_to([128, 2, 64]), [], ["wqk_k"], nonc=True)
    dma("sp", gnwb, gnw.partition_broadcast(128), [], ["gnwb"], nonc=True)
    dma("sp", esink, sinks.partition_broadcast(128), [], ["esink"], nonc=True)
    for (dst, key, blk) in ((g1P, "g1P", 2), (g1S, "g1S", 2), (g2P, "g2P", 5), (g2S, "g2S", 5)):
        for half in range(2):
            dma("sp", dst[:, half * 512:(half + 1) * 512],
                b_ada[0, blk * 1024 + half * 512: blk * 1024 + (half + 1) * 512].partition_broadcast(128), [], [key + str(half)], nonc=True)
    def load_x(i):
        src = xp[i * 128:(i + 1) * 128, :] if i < 16 else xs
        dma("sp", h[:, i, :], src, ["h%d" % (i - 2)] if i >= 2 else [], ["h%d" % i])

    load_x(0); load_x(1)
    w_ada_v = w_ada.rearrange("(k p) n -> p k n", p=128)
    w_in_v = w_in.rearrange("(k p) n -> p k n", p=128)

    def load_wada(hb):
        dma("pool", wada[hb % NWA], w_ada_v[:, :, hb * 512:(hb + 1) * 512], [], ["wada%d" % (hb % NWA)])

    load_wada(0); load_wada(1); load_wada(2)
    dma("pool", win[:, :, 0:1536], w_in_v[:, :, 0:1536], [], ["win_a"])
    dma("pool", win[:, :, 1536:2304], w_in_v[:, :, 1552:2320], [], ["win_b"])
    dma("pool", bgrow, b_gate, [], ["bgrow"])

    P.add("dve", lambda e: e.memset(ssa, 0.0), writes=["ssa"])
    P.add("dve", lambda e: e.memset(ssb, 0.0), writes=["ssb"])

    act(csb2, csb, AF.Exp, ["csb"], ["csb2"], scale=-1.0)
    act(csb2, csb2, AF.Ln, ["csb2"], ["csb2"], bias=1.0, scale=1.0)
    act(csb2, csb2, AF.Exp, ["csb2"], ["csb2"], scale=-1.0)
    tt(scb, csb, csb2, ALU.mult, ["csb", "csb2"], ["scb"])
    pT = pbf(5)
    for k in range(8):
        tr(pT[:, k * 32:k * 32 + 17], scb[:, k * 128:(k + 1) * 128], ident[0:17, 0:17], ["scb", "cb"], ["ps5"])
    cpy("dve", scT, pT[:, 0:256].rearrange("p (k c) -> p k c", k=8)[:, :, 0:17], ["ps5"], ["scT"])
    cpy("dve", scP, scT[:, :, 0:1].broadcast_to([128, 8, 128]), ["scT"], ["scP"])
    cpy("dve", scS.rearrange("p k (b t) -> p k b t", t=8),
        scT[:, :, 1:17].rearrange("p k (b o) -> p k b o", o=1).broadcast_to([128, 8, 16, 8]), ["scT"], ["scS"])

    for k in range(8):
        tr(psf[0:16, 3072 + k * 128:3072 + (k + 1) * 128], glr[:, k, :], ident32, ["glr", "cf"], ["ps6", "ps7"])
    cpy("act", glrT.rearrange("p k d -> p (k d)"), psf[0:16, 3072:4096], ["ps6", "ps7"], ["glrT"])
    for k in range(8):
        mm(ps[:, k % 2, 0:256], glrT[:, k, :], wgus, True, True, ["glrT", "wgus"], ["ps%d" % (k % 2)])
        cpy("dve", win[:, k, 2304:2560], ps[:, k % 2, 0:256], ["ps%d" % (k % 2)], ["win_g%d" % k])

    def fm_half(hb, dst, dkey):
        blk, hf = hb // 2, hb % 2
        sl = wada[hb % NWA]; skey = "wada%d" % (hb % NWA)
        for c4 in range(4):
            cc = hf * 4 + c4
            for k in range(8):
                mm(ps[:, 4, cc * 17:(cc + 1) * 17], sl[:, k, c4 * 128:(c4 + 1) * 128], scT[:, k, :], k == 0, k == 7, [skey, "scT"], ["ps4"])
        tt(dst[:, hf * 4:(hf + 1) * 4, :], ps[:, 4, hf * 68:(hf + 1) * 68].rearrange("p (k c) -> p k c", k=4),
           bfm[:, blk * 8 + hf * 4: blk * 8 + (hf + 1) * 4].rearrange("p (k o) -> p k o", o=1).broadcast_to([128, 4, 17]), ALU.add,
           ["ps4", "bfm"], [dkey + str(hf)])

    def tm_half(hb, dstP, dstS, keyP, keyS):
        blk, hf = hb // 2, hb % 2
        sl = wada[hb % NWA]; skey = "wada%d" % (hb % NWA)
        for (lhs, lk, dst, dk, bank) in ((scP, "scP", dstP, keyP, 0), (scS, "scS", dstS, keyS, 1)):
            for k in range(8):
                mm(ps[:, bank, :], lhs[:, k, :], sl[:, k, :], k == 0, k == 7, [skey, lk], ["ps%d" % bank])
            dv = dst[:, hf * 512:(hf + 1) * 512]
            tt(dv, ps[:, bank, :], dv, ALU.add, ["ps%d" % bank, dk + str(hf)], [dk + str(hf)])

    for hb in range(12):
        blk = hb // 2
        if blk == 0:
            fm_half(hb, sh1, "sh1_")
        elif blk == 1:
            fm_half(hb, modraw[0], "mr0_")
        elif blk == 2:
            tm_half(hb, g1P, g1S, "g1P", "g1S")
        elif blk == 3:
            fm_half(hb, sh2, "sh2_")
        elif blk == 4:
            fm_half(hb, modraw[1], "mr1_")
        else:
            tm_half(hb, g2P, g2S, "g2P", "g2S")
        if hb + NWA < 12:
            load_wada(hb + NWA)
    for (mr, mk, nf, nk, dst, dk) in ((modraw[0], "mr0_", n1f, "n1f", a1, "a1"), (modraw[1], "mr1_", n2f, "n2f", a2, "a2")):
        ts(mr, mr, 1.0, ALU.add, [mk + "0", mk + "1"], [mk + "0", mk + "1"])
        tt(dst, mr, nf.rearrange("p (k o) -> p k o", o=1).broadcast_to([128, 8, 17]), ALU.mult, [mk + "0", mk + "1", nk], [dk])
    SH1K = ["sh1_0", "sh1_1"]; SH2K = ["sh2_0", "sh2_1"]

    tt(wtmp[:, 0:64], wqk[:, 0:64], wqk[:, 0:64], ALU.mult, ["wqk_q"], ["wtmp"])
    P.add("dve", lambda e: e.tensor_reduce(out=negc[:, 1:2], in_=wtmp[:, 0:64], axis=AX.X, op=ALU.max), reads=["wtmp"], writes=["negc1"])
    tt(wtmp[:, 0:64], wqk[:, 512:576], wqk[:, 512:576], ALU.mult, ["wqk_k", "negc1"], ["wtmp"])
    P.add("dve", lambda e: e.tensor_reduce(out=negc[:, 2:3], in_=wtmp[:, 0:64], axis=AX.X, op=ALU.max), reads=["wtmp"], writes=["negc2"])
    tt(negc[:, 3:4], negc[:, 1:2], negc[:, 2:3], ALU.mult, ["negc1", "negc2"], ["negc3"])
    act(negc[:, 4:5], negc[:, 3:4], AF.Ln, ["negc3"], ["negc4"])
    act(negc[:, 5:6], negc[:, 4:5], AF.Exp, ["negc4"], ["negc5"], scale=0.5)
    ts(negc[:, 0:1], negc[:, 5:6], -8.0, ALU.mult, ["negc5"], ["negc"])
    act(esink, esink, AF.Exp, ["esink", "negc"], ["esink"], bias=negc[:, 0:1], scale=1.0)

    P.fence("dve", lambda e: e.memset(smallf[:, 200:201], 0.0))
    dma("pool", wout, w_out.rearrange("(k p) n -> p k n", p=128), [], ["wout"])
    def scratch_dma(q8):
        if q8 < 4:
            q = q8
            dma("pool", w1s[q * 256:(q + 1) * 256, :], w1[q * 256:(q + 1) * 256, :], ["wscr"], ["w1s%d" % q, "wscr"], dsem="d_w1s%d" % q)
        else:
            q = q8 - 4
            dma("pool", w2s[q * 1024:(q + 1) * 1024, :], w2[q * 1024:(q + 1) * 1024, :], ["wscr"], ["w2s%d" % q, "wscr"], dsem="d_w2s%d" % q)

    W1S_KEYS = ["w1s%d" % q for q in range(4)]
    W2S_KEYS = ["w2s%d" % q for q in range(4)]

    P.add("dve", lambda e: e.memset(S32, 0.0), writes=["S32"])
    for q in range(3):
        P.add("dve", lambda e, q=q: e.memset(vaug[q], 1.0), writes=["vaug%d" % q])
    P.add("dve", lambda e: e.memset(vca, 1.0), writes=["vca"])
    dma("pool", ks[:, 0:120, :], ck_in[:, 8:128, :], [], ["ks_past"])
    dma("pool", vs[:, 0:120, :], cv_in[:, 8:128, :], [], ["vs_past"])

    WIN_KEYS = ["win_a", "win_b"] + ["win_g%d" % k for k in range(8)]

    def mixer_s1(i, part):
        SMP = (i == 16)
        par = i % 2
        hi = h[:, i, :]
        hk = "h%d" % i
        rp = ropeR[par]; rk = "rope%d" % par
        mix = mixR[par]; mixk = "mix%d" % par
        rotb = rotbR[par]; rotk = "rotb%d" % par
        vq = i % 3
        ebl = eblR[par]; eblk = "ebl%d" % par
        OA, OB = (3, 4) if SMP else (0, 2)
        if part == "Nn":
            dma("sp", rp, cst_f[:, CF_ROPE + i * 128: CF_ROPE + (i + 1) * 128], [], [rk])
            if i + 2 < NT:
                load_x(i + 2)
            if 2 <= i < 10:
                scratch_dma(i - 2)

        def modb(m):
            if SMP:
                return m[:, :, 1:17].rearrange("p k (b o) -> p k b o", o=1).broadcast_to([128, 8, 16, 8])
            return m[:, :, 0:1].broadcast_to([128, 8, 128])

        def fmv(t):
            if SMP:
                return t.rearrange("p k (b t) -> p k b t", t=8)
            return t

        pT = pbf(5)
        if part == "Nn":
            tb = tbuf
            act(tb, hi, AF.Square, [hk, "ssa"], ["tbuf", "ssa%d" % i], accum=ssa[:, i:i + 1])
            rstd_from_ss(ssa[:, i:i + 1], 1, smallf[:, 0:1], smallf[:, 1:2], 1024.0, ["ssa%d" % i], "rs1")
            ts(tb, hi, smallf[:, 0:1], ALU.mult, [hk, "rs1"], ["tbuf"])
            pT = pbf(5)
            P.begin_atomic()
            for k in range(8):
                tr(pT[:, k * 128:(k + 1) * 128], tb[:, k * 128:(k + 1) * 128], ident, ["tbuf", "cb"], ["ps5"])
            pTv = pT.rearrange("p (k t) -> p k t", k=8)
            tt(fmv(hnT), fmv(pTv), modb(a1), ALU.mult, ["ps5", "a1"], ["hnT"])
            P.end_atomic()
            tt(fmv(hnT), fmv(hnT), modb(sh1), ALU.add, ["hnT"] + SH1K, ["hnT"])
            return
        def proj_bank(bank):
            for k in range(8):
                mm(ps[:, bank, :], hnT[:, k, :], win[:, k, bank * 512:(bank + 1) * 512], k == 0, k == 7,
                   ["hnT"] + WIN_KEYS, ["ps%d" % bank])

        tri = cf[:, CF_TRIS:CF_TRIS + 128] if SMP else cf[:, CF_TRIP:CF_TRIP + 128]
        NB = 16 if SMP else 1
        negsel = cf[:, CF_NEGS:CF_NEGS + 16] if SMP else cf[:, CF_NEGP:CF_NEGP + 1]
        eb = e1
        if part == "Npa":
            for k in range(8):
                mm(ps[:, 4, 256:512], hnT[:, k, :], win[:, k, 2304:2560], k == 0, False, ["hnT"] + WIN_KEYS, ["ps4"])
            mm(ps[:, 4, 256:512], ones_row[0:1, 0:128], bgrow[0:1, :], False, True, ["bgrow", "cb"], ["ps4"])
            act(e1, ps[:, 4, 256:512], AF.Exp, ["ps4"], ["e1"], scale=-1.0)
            act(spb, e1, AF.Ln, ["e1"], ["spb"], bias=1.0, scale=1.0)
            proj_bank(3)
            mm(ps[:, 4, 0:256], tri, spb, True, True, ["spb", "cf"], ["ps4"])
            for j in range(2):
                mm(ps[:, 4, 256 + j * NB:256 + (j + 1) * NB], spb[:, j * 128:(j + 1) * 128], negsel, True, True, ["spb", "cf"], ["ps4"])
            act(eb, ps[:, 4, 0:256], AF.Exp, ["ps4"], ["e1"], bias=math.log(0.125), scale=1.0)
            act(enb, ps[:, 4, 0:256], AF.Exp, ["ps4"], ["enb"], scale=-1.0)
            act(ebl[:, :, 0:NB], ps[:, 4, 256:256 + 2 * NB].rearrange("p (j b) -> p j b", j=2), AF.Exp, ["ps4"], [eblk])
            for k in range(8):
                mm(ps[:, 4, 0:256], hnT[:, k, :], win[:, k, 2048:2304], k == 0, k == 7, ["hnT"] + WIN_KEYS, ["ps4"])
            return
        if part == "Npb":
            proj_bank(0)
            proj_bank(1)
            proj_bank(2)
            tt(qkt[:, 0:256], ps[:, 0, 0:256], eb, ALU.mult, ["ps0", "e1"], ["qkt_q"])
            tt(qkt[:, 256:512], ps[:, 0, 256:512], enb, ALU.mult, ["ps0", "enb"], ["qkt_k"])
            cpy("act", vb, ps[:, 1, :], ["ps1"], ["vb"])
            P.begin_atomic()
            for c in range(4):
                tr(pT[:, c * 128:(c + 1) * 128], qkt[:, c * 128:(c + 1) * 128], ident, ["qkt_q", "qkt_k", "cb"], ["ps5"])
            cpy("act", qkT, pT[:, 0:512].rearrange("p (c t) -> p c t", c=4), ["ps5"], ["qkT"])
            P.end_atomic()
            return
        qkps = psf[:, 1536:2176]
        sq2 = r1
        v10 = lambda t_: t_.rearrange("p (h d) -> p h d", h=10)
        gate = gateb
        rot32 = r1
        if part == "B1g":
            act(gate, ps[:, 2, :], AF.Exp, ["ps2"], ["gateb"], scale=-1.0)
            act(gate, gate, AF.Ln, ["gateb"], ["gateb"], bias=1.0, scale=1.0)
            act(gate, gate, AF.Exp, ["gateb"], ["gateb"], scale=-1.0)
            tt(gate, ps[:, 2, :], gate, ALU.mult, ["ps2", "gateb"], ["gateb"])
            tt(gate.rearrange("p (h d) -> p h d", h=4), gate.rearrange("p (h d) -> p h d", h=4),
               gnwb.rearrange("p (o d) -> p o d", o=1).broadcast_to([128, 4, 128]), ALU.mult, ["gateb", "gnwb"], ["gateb"])
            return
        if part == "B1q":
            act(sq2, qkps, AF.Square, ["ps3", "ps4"], ["r1"])
            P.add("dve", lambda e: e.tensor_reduce(out=smallf[:, 24:34], in_=sq2.rearrange("p (h d) -> p h d", h=10), axis=AX.X, op=ALU.add),
                  reads=["r1"], writes=["ssq"])
            rstd_from_ss(smallf[:, 24:34], 10, smallf[:, 40:50], smallf[:, 56:66], 64.0, ["ssq"], "rqk")
            tt(v10(qkn), v10(qkps), smallf[:, 40:50].rearrange("p (h o) -> p h o", o=1).broadcast_to([128, 10, 64]), ALU.mult,
               ["ps3", "ps4", "rqk"], ["qkn"])
            cpy("dve", vaug[vq][:, :, 0:64], ps[:, 4, 128:256].rearrange("p (g d) -> p g d", g=2), ["ps4"], ["vaug%d" % vq])
            if i == 15 or SMP:
                cpy("act", sv32, ps[:, 4, 128:256], ["ps4"], ["sv32"])
            return
        if part == "B2":
            PE_ = os.environ.get("KPOOL", "1") == "1" and "pool" or "dve"
            tt(qkn, qkn, wqk, ALU.mult, ["qkn", "wqk_q", "wqk_k"], ["qkn"], eng=PE_)
            cosb = rp[:, 0:64].rearrange("p (o d) -> p o d", o=1).broadcast_to([128, 10, 64])
            sin_lo = rp[:, 64:96].rearrange("p (o d) -> p o d", o=1).broadcast_to([128, 10, 32])
            sin_hi = rp[:, 96:128].rearrange("p (o d) -> p o d", o=1).broadcast_to([128, 10, 32])
            r2 = gr2
            tt(v10(r1), v10(qkn), cosb, ALU.mult, ["qkn", rk], ["r1"], eng=PE_)
            tt(v10(r2)[:, :, 0:32], v10(qkn)[:, :, 32:64], sin_lo, ALU.mult, ["qkn", rk], ["gr2"], eng=PE_)
            tt(v10(r2)[:, :, 32:64], v10(qkn)[:, :, 0:32], sin_hi, ALU.mult, ["qkn", rk], ["gr2"], eng=PE_)
            tt(rot32, r1, r2, ALU.add, ["r1", "gr2"], ["r1"], eng=PE_)
            cpy("act", rotb[:, 0:512].rearrange("p (j g d) -> p g j d", j=4, g=2), rot32[:, 0:512].rearrange("p (g j d) -> p g j d", g=2, j=4),
                ["r1"], [rotk])
            cpy("act", rotb[:, 512:640], rot32[:, 512:640], ["r1"], [rotk])
            if i == 15:
                dma("pool", kp, rot32[:, 512:640], ["r1"], ["kp"])
                dma("pool", vp, sv32, ["sv32"], ["vp"])
            elif SMP:
                dma("pool", ks[:, 120:128, :], rot32[:, 512:640], ["r1"], ["ks_new"])
                dma("pool", vs[:, 120:128, :], sv32[:, :], ["sv32"], ["vs_new"])
            return

        for hh in range(4):
            p_, j = hh % 2, hh // 2
            ob_ = OA if p_ == 0 else OB
            mm(ps[:, ob_, j * 128:(j + 1) * 128], qkT[64 * p_:64 * p_ + 64, 2 + j, :], qkT[64 * p_:64 * p_ + 64, j, :], True, True,
               ["qkT"], ["ps%d" % ob_])
        cm = cb[:, CB_CMS:CB_CMS + 128] if SMP else cb[:, CB_CMP:CB_CMP + 128]
        attm = attmb.rearrange("p (h t) -> p h t", h=4)
        for q_ in range(2):
            ob_ = OA if q_ == 0 else OB
            tt(attm.rearrange("p (j q) t -> p q j t", q=2)[:, q_], ps[:, ob_, 0:256].rearrange("p (j t) -> p j t", j=2),
               cm.rearrange("p (o t) -> p o t", o=1).broadcast_to([128, 2, 128]),
               ALU.mult, ["ps%d" % ob_, "cb"], ["attm%d" % q_])

        if SMP:
            for grp in range(8):
                for p_ in range(2):
                    src = st_in[2 * grp:2 * grp + 2].rearrange("b (j p) d v -> p d b j v", p=2)[p_]
                    dma("sp", Sst[64 * p_:64 * p_ + 64], src, [], ["Sst_%d" % p_])
                cpy("act", Sstb, Sst, ["Sst_0", "Sst_1"], ["Sstb"])
                for bl in range(2):
                    b = 2 * grp + bl
                    for hh in range(4):
                        p_, j = hh % 2, hh // 2
                        ob = 0 if p_ == 0 else 2
                        mm(ps[:, ob, j * 128 + b * 8: j * 128 + b * 8 + 8], Sstb[64 * p_:64 * p_ + 64, bl, j, :],
                           qkT[64 * p_:64 * p_ + 64, j, b * 8:(b + 1) * 8], True, True, ["Sstb", "qkT"], ["ps%d" % ob])
                for bl in range(2):
                    b = 2 * grp + bl
                    ts(km, qkt[:, 256:512], cf[:, CF_ROWSEL + b:CF_ROWSEL + b + 1], ALU.mult, ["qkt_k", "cf"], ["km"])
                    for j in range(2):
                        mm(ps[:, 1, j * 256:(j + 1) * 256], km[:, j * 128:(j + 1) * 128], vb[:, j * 256:(j + 1) * 256], True, True,
                           ["km", "vb"], ["ps1"])
                    dsv = ps[:, 1, :].rearrange("p (j c) -> p j c", j=2)
                    for p_ in range(2):
                        rs = slice(64 * p_, 64 * p_ + 64)
                        tt(tmpS[rs], dsv[rs, :, 128 * p_:128 * p_ + 128], Sst[rs, bl, :, :], ALU.add,
                           ["ps1", "Sst_0", "Sst_1"], ["tmpS%d" % p_])
                        tt(Sout[rs, bl, :, :], tmpS[rs], ebl[rs, :, b:b + 1].broadcast_to([64, 2, 128]), ALU.mult,
                           ["tmpS%d" % p_, eblk], ["Sout_%d_%d" % (bl, p_)])
                for p_ in range(2):
                    dst = gs[2 * grp:2 * grp + 2].rearrange("b (j p) d v -> p d b j v", p=2)[p_]
                    dma("pool", dst, Sout[64 * p_:64 * p_ + 64], ["Sout_%d_%d" % (bl, p_) for bl in range(2)],
                        ["gs_%d_%d" % (grp, p_)], dsem="d_Sout_%d" % p_)
            for q_ in range(2):
                ob = 0 if q_ == 0 else 2
                cpy("act", oTb.rearrange("p (j q) t -> p q j t", q=2)[:, q_], ps[:, ob, 0:256].rearrange("p (j t) -> p j t", j=2),
                    ["ps%d" % ob], ["oTb%d" % q_])

        for hh in range(4):
            p_, j = hh % 2, hh // 2
            last = (i == 0)
            ob_ = OA if p_ == 0 else OB
            ok_ = "ps%d" % ob_
            mm(ps[:, ob_, j * 128:(j + 1) * 128], attm[:, hh, :], vb[:, hh * 128:(hh + 1) * 128], True, last, ["attm0", "attm1", "vb"], [ok_])
            if SMP:
                mm(ps[:, ob_, j * 128:(j + 1) * 128], oTb[:, hh, :], ident, False, True, ["oTb0", "oTb1", "cb"], [ok_])
            elif i > 0:
                mm(ps[:, ob_, j * 128:(j + 1) * 128], qkT[64 * p_:64 * p_ + 64, j, :], Sbf[64 * p_:64 * p_ + 64, j, :], False, True,
                   ["qkT", "Sbf"], [ok_])
        if not SMP:
            for j in range(2):
                mm(ps[:, 1, j * 256:(j + 1) * 256], qkt[:, 256 + j * 128:256 + (j + 1) * 128], vb[:, j * 256:(j + 1) * 256], True, True,
                   ["qkt_k", "vb"], ["ps1"])
            dsv = ps[:, 1, :].rearrange("p (j c) -> p j c", j=2)
            for p_ in range(2):
                rs = slice(64 * p_, 64 * p_ + 64)
                tt(tmpS[rs], dsv[rs, :, 128 * p_:128 * p_ + 128], S32[rs], ALU.add, ["ps1", "S32"], ["tmpS%d" % p_])
                tt(S32[rs], tmpS[rs], ebl[rs, :, 0:1].broadcast_to([64, 2, 128]), ALU.mult, ["tmpS%d" % p_, eblk], ["S32"])
            cpy("act", Sbf, S32, ["S32"], ["Sbf"])
            if i == 15:
                for p_ in range(2):
                    dma("pool", gp.rearrange("(j p) d v -> p d j v", p=2)[p_], S32[64 * p_:64 * p_ + 64], ["S32"], ["gp%d" % p_])

        act(osqb[:, 0:256], ps[:, OA, 0:256], AF.Square, ["ps%d" % OA], ["osq0"])
        act(osqb[:, 256:512], ps[:, OB, 0:256], AF.Square, ["ps%d" % OB], ["osq1"])
        P.add("dve", lambda e: e.tensor_reduce(out=smallf[:, 8:12], in_=osqb.rearrange("p (h d) -> p h d", h=4), axis=AX.X, op=ALU.add),
              reads=["osq0", "osq1"], writes=["ssg"])
        rstd_from_ss(smallf[:, 8:12], 4, smallf[:, 12:16], smallf[:, 16:20], 128.0, ["ssg"], "rsg")
        for hh in range(4):
            p_, j = hh % 2, hh // 2
            ri = 12 + p_ * 2 + j
            P.add("dve", lambda e, hh=hh, p_=p_, j=j, ri=ri: e.scalar_tensor_tensor(
                out=mix[:, hh * 128:(hh + 1) * 128], in0=ps[:, (OA if p_ == 0 else OB), j * 128:(j + 1) * 128],
                scalar=smallf[:, ri:ri + 1], in1=gate[:, hh * 128:(hh + 1) * 128], op0=ALU.mult, op1=ALU.mult),
                  reads=["ps%d" % OA, "ps%d" % OB, "rsg", "gateb"], writes=[mixk + "g"])


    def mixer_s2(i):
        SMP = (i == 16)
        par = i % 2
        hk = "h%d" % i
        mix = mixR[par]; mixk = "mix%d" % par
        rotb = rotbR[par]; rotk = "rotb%d" % par
        vq = i % 3; vpq = (i - 1) % 3
        cur = i % 2; prv = 1 - cur
        pT = pbf(5)
        qT = qTb
        P.begin_atomic()
        for c in range(5):
            tr(pT[:, c * 128:(c + 1) * 128], rotb[:, c * 128:(c + 1) * 128], ident, [rotk, "cb"], ["ps5"])
        cpy("act", qT, pT[:, 0:512].rearrange("p (c t) -> p c t", c=4), ["ps5"], ["qT"])
        cpy("act", kTr[cur], pT[:, 512:640], ["ps5"], ["kTr%d" % cur])
        P.end_atomic()
        blocks = []
        if (not SMP) and i > 0:
            blocks.append((kTr[prv], "kTr%d" % prv, vaug[vpq], "vaug%d" % vpq, CB_MASKB, 0))
        blocks.append((kTr[cur], "kTr%d" % cur, vaug[vq], "vaug%d" % vq, CB_MASKB + (1024 if SMP else 512), 1))
        osw = ps[:, 6:8, 0:260].rearrange("p g (j e) -> p g j e", e=65)
        Eg = [[None, None], [None, None]]
        if SMP:
            Esm = [(E2[0], "E2_0"), (mixT.rearrange("p k t -> p (k t)")[:, 0:512], "mixT")]
        for g in range(2):
            for (kt, ktk, va, vak, mo, bi) in blocks:
                bank = 6 + bi
                mm(ps[:, bank, :], kt[64 * g:64 * g + 64, :], qT[64 * g:64 * g + 64, :, :], True, False, [ktk, "qT"], ["ps%d" % bank])
                mm(ps[:, bank, :], ident, cb[:, mo:mo + 512], False, True, ["cb"], ["ps%d" % bank])
                dstE = E2[bi] if g == 0 else (mixT.rearrange("p k t -> p (k t)")[:, bi * 512:(bi + 1) * 512])
                dk = ("E2_%d" % bi) if g == 0 else "mixT"
                act(dstE, ps[:, bank, :], AF.Exp, ["ps%d" % bank, "negc"], [dk], bias=negc[:, 0:1], scale=0.125)
                Eg[g][bi] = (dstE, dk)
        if SMP:
            for r in range(2):
                Ecr, Eck = Esm[r]
                dma("pool", ckb, ck_in[8 * r:8 * r + 8].rearrange("b c e -> c b e"), [], ["ckb"])
                for g in range(2):
                    dma("pool", vca[:, :, g, 0:64], cv_in[8 * r:8 * r + 8, :, g * 64:(g + 1) * 64].rearrange("b c d -> c b d"), [], ["vca"],
                        dsem="d_vca%d" % g)
                P.begin_atomic()
                for bb in range(8):
                    tr(pT[:, bb * 128:(bb + 1) * 128], ckb[:, bb, :], ident, ["ckb", "cb"], ["ps5"])
                cpy("act", kcT, pT.rearrange("p (b c) -> p b c", b=8), ["ps5"], ["kcT"])
                P.end_atomic()
                for bb in range(8):
                    b = r * 8 + bb
                    for g in range(2):
                        mm(ps[:, 6 + g, bb * 32: bb * 32 + 32], kcT[64 * g:64 * g + 64, bb, :],
                           qT[64 * g:64 * g + 64, :, b * 8:(b + 1) * 8], True, True, ["kcT", "qT"], ["ps%d" % (6 + g)])
                for g in range(2):
                    act(Ecr[:, g * 256:(g + 1) * 256], ps[:, 6 + g, 0:256], AF.Exp, ["ps%d" % (6 + g), "negc"], [Eck], bias=negc[:, 0:1], scale=0.125)
                tt(Ecr.rearrange("p (x t) -> p x t", t=8), Ecr.rearrange("p (x t) -> p x t", t=8),
                   cb[:, CB_CMC:CB_CMC + 8].rearrange("p (o t) -> p o t", o=1).broadcast_to([128, 64, 8]), ALU.mult,
                   [Eck, "cb"], [Eck])
                for bb in range(8):
                    for g in range(2):
                        mm(ps[0:65, 6 + g, bb * 32: bb * 32 + 32], vca[:, bb, g, :],
                           Ecr[:, g * 256 + bb * 32: g * 256 + bb * 32 + 32], True, True, ["vca", Eck], ["ps%d" % (6 + g)])
                for g in range(2):
                    cpy("act", oTc[0:65, g, :, r * 64:(r + 1) * 64].rearrange("p j (b t) -> p b j t", t=8),
                        ps[0:65, 6 + g, 0:256].rearrange("p (b j t) -> p b j t", b=8, j=4),
                        ["ps%d" % (6 + g)], ["oTc%d_%d" % (r, g)])
        for g in range(2):
            for j in range(4):
                out = ps[:, 6 + g, j * 65:(j + 1) * 65]
                n = len(blocks) + (1 if SMP else 0)
                c_ = 0
                for (kt, ktk, va, vak, mo, bi) in blocks:
                    dE, dk = Eg[g][bi]
                    mm(out, dE[:, j * 128:(j + 1) * 128], va[:, g, :], c_ == 0, c_ == n - 1, [dk, vak], ["ps%d" % (6 + g)])
                    c_ += 1
                if SMP:
                    mm(out, oTc[0:65, g, j, :], ident[0:65, 0:65], False, True, ["oTc0_0", "oTc0_1", "oTc1_0", "oTc1_1", "cb"], ["ps%d" % (6 + g)])
        den = smallf[:, 72:80].rearrange("p (g j) -> p g j", g=2)
        rden = smallf[:, 80:88].rearrange("p (g j) -> p g j", g=2)
        tt(den, osw[:, :, :, 64], esink.rearrange("p (g j) -> p g j", g=2), ALU.add, ["ps6", "ps7", "esink"], ["den"])
        P.add("dve", lambda e: e.reciprocal(out=rden, in_=den), reads=["den"], writes=["rden"])
        tt(mix[:, 512:1024].rearrange("p (g j d) -> p g j d", g=2, j=4), osw[:, :, :, 0:64],
           smallf[:, 80:88].rearrange("p (g j o) -> p g j o", g=2, o=1).broadcast_to([128, 2, 4, 64]), ALU.mult,
           ["ps6", "ps7", "rden"], [mixk + "s"])

        P.begin_atomic()
        for e_ in range(8):
            tr(pT[:, e_ * 128:(e_ + 1) * 128], mix[:, e_ * 128:(e_ + 1) * 128], ident, [mixk + "g", mixk + "s", "cb"], ["ps5"])
        cpy("act", mixT, pT.rearrange("p (k t) -> p k t", k=8), ["ps5"], ["mixT"])
        P.end_atomic()
        g1b = g1S if SMP else g1P
        for half in range(2):
            for e_ in range(8):
                mm(ps[:, 6 + half, :], mixT[:, e_, :], wout[:, e_, half * 512:(half + 1) * 512], e_ == 0, e_ == 7, ["mixT", "wout"], ["ps%d" % (6 + half)])
            g1k = ("g1S%d" if SMP else "g1P%d") % half
            hv = h[:, i, half * 512:(half + 1) * 512]
            tt(tmpf, ps[:, 6 + half, :], g1b[:, half * 512:(half + 1) * 512], ALU.mult, ["ps%d" % (6 + half), g1k], ["tmpf"])
            tt(hv, hv, tmpf, ALU.add, [hk, "tmpf"], [hk])

    KSTOP = os.environ.get("KSTOP", "")
    if KSTOP == "pro":
        P.emit(nc)
        return nc
    order = list(range(NT))
    if KSTOP.startswith("a") and len(KSTOP) > 1:
        order = order[:int(KSTOP[1:])]
    if KSTOP.startswith("s"):
        order = [16]
    PIPE = os.environ.get("KPIPE", "1") == "1"
    prev_s2 = None
    for i in order:
        if i == order[0]:
            for pn in ("Nn", "Npa", "B1q"):
                P.flush_positions([(P.capture(mixer_s1, i, pn), 0.0, 1.0)])
        sNb = P.capture(mixer_s1, i, "Npb")
        sBg = P.capture(mixer_s1, i, "B1g")
        sA2 = P.capture(mixer_s1, i, "A2")
        sB2 = P.capture(mixer_s1, i, "B2")
        nxt = order[order.index(i) + 1] if order.index(i) + 1 < len(order) else None
        if nxt is not None:
            for pn in ("Nn", "Npa", "B1q"):
                sB2 = sB2 + P.capture(mixer_s1, nxt, pn)
        X = sNb + sBg + sA2
        if PIPE:
            f1 = (len(sNb) + len(sBg) + 0.5) / len(X)
            streams = [(X, 0.0, 1.0), (sB2, f1, 1.0)]
            if prev_s2 is not None:
                streams.append((prev_s2, 0.0, 1.0))
            P.flush_positions(streams)
        else:
            if prev_s2 is not None:
                P.flush_positions([(prev_s2, 0.0, 1.0)])
            P.flush_positions([(X, 0.0, 1.0)]); P.flush_positions([(sB2, 0.0, 1.0)])
        prev_s2 = P.capture(mixer_s2, i)
    P.flush_positions([(prev_s2, 0.0, 1.0)])
    if KSTOP.startswith("a") or KSTOP.startswith("s"):
        dbg("h", h, ["h%d" % i for i in range(NT)])
        P.emit(nc)
        return nc

    w1_cnt = [0]; w2_cnt = [0]; w1_pre = [False]

    def mlp_norm(j):
        tiles = ST_TILES[j]
        hn2 = hn2TR[j % 2]
        for li, i in enumerate(tiles):
            SMP = (i == 16)
            hi = h[:, i, :]; hk = "h%d" % i
            act(t2, hi, AF.Square, [hk, "ssb"], ["t2", "ssb%d" % i], accum=ssb[:, i:i + 1])
            rstd_from_ss(ssb[:, i:i + 1], 1, smallf[:, 100:101], smallf[:, 101:102], 1024.0, ["ssb%d" % i], "rs2")
            ts(t2, hi, smallf[:, 100:101], ALU.mult, [hk, "rs2"], ["t2"])
            pT = pbf(7)
            P.begin_atomic()
            for k in range(8):
                tr(pT[:, k * 128:(k + 1) * 128], t2[:, k * 128:(k + 1) * 128], ident, ["t2", "cb"], ["ps7"])
            pTv = pT.rearrange("p (k t) -> p k t", k=8)
            dstv = hn2[:, :, li * 128:(li + 1) * 128]
            if SMP:
                ma = a2[:, :, 1:17].rearrange("p k (b o) -> p k b o", o=1).broadcast_to([128, 8, 16, 8])
                ms = sh2[:, :, 1:17].rearrange("p k (b o) -> p k b o", o=1).broadcast_to([128, 8, 16, 8])
                dv = dstv.rearrange("p k (b t) -> p k b t", t=8); sv_ = pTv.rearrange("p k (b t) -> p k b t", t=8)
            else:
                ma = a2[:, :, 0:1].broadcast_to([128, 8, 128]); ms = sh2[:, :, 0:1].broadcast_to([128, 8, 128])
                dv = dstv; sv_ = pTv
            tt(dv, sv_, ma, ALU.mult, ["ps7", "a2"], ["hn2T%d_%d" % (j % 2, li)])
            P.end_atomic()
            tt(dv, dv, ms, ALU.add, ["hn2T%d_%d" % (j % 2, li)] + SH2K, ["hn2T%d_%d" % (j % 2, li)])

    def mlp_ff1(j):
        tiles = ST_TILES[j]
        T = 128 * len(tiles)
        hn2 = hn2TR[j % 2]
        hnk = ["hn2T%d_%d" % (j % 2, li) for li in range(len(tiles))]
        for s8 in range(8):
            slot = w1_cnt[0] % 3; w1_cnt[0] += 1
            if not (j == 0 and s8 < 3 and w1_pre[0]):
                dma("sp", W1R[slot], w1s.rearrange("(k p) n -> p k n", p=128)[:, :, s8 * 512:(s8 + 1) * 512], W1S_KEYS, ["W1R%d" % slot])
            for fl in range(4):
                fc = s8 * 4 + fl
                bk = fc % 2
                chunks = [(0, min(T, 512))] + ([(512, T)] if T > 512 else [])
                for (c0, c1) in chunks:
                    bank = (0 if bk == 0 else 2) + (0 if c0 == 0 else 1)
                    for k in range(8):
                        mm(ps[:, bank, 0:c1 - c0], W1R[slot][:, k, fl * 128:(fl + 1) * 128], hn2[:, k, c0:c1], k == 0, k == 7,
                           ["W1R%d" % slot] + hnk, ["ps%d" % bank])
                b0 = 0 if bk == 0 else 2
                act(rl[bk][:, 0:T], psf[:, b0 * 512:b0 * 512 + T], AF.Relu, ["ps%d" % b0, "ps%d" % (b0 + 1)], ["rl%d" % bk])
                tt(hT[:, fc, 0:T], rl[bk][:, 0:T], rl[bk][:, 0:T], ALU.mult, ["rl%d" % bk], ["hT_%d" % fc])

    def mlp_ff2(j):
        tiles = ST_TILES[j]
        hTk = ["hT_%d" % fc for fc in range(32)]
        g2k = {True: ["g2S0", "g2S1"], False: ["g2P0", "g2P1"]}
        for half in range(2):
            banks = [2, 3, 4, 5, 6] if half == 0 else [0, 1, 2, 3, 4]
            for s8 in range(8):
                slot = w2_cnt[0] % 4; w2_cnt[0] += 1
                dma("sp", W2R[slot], w2s.rearrange("(c p) n -> p c n", p=128)[:, s8 * 4:(s8 + 1) * 4, half * 512:(half + 1) * 512],
                    W2S_KEYS, ["W2R%d" % slot])
                for li, i in enumerate(tiles):
                    bank = banks[li]
                    for fl in range(4):
                        fc = s8 * 4 + fl
                        mm(ps[:, bank, :], hT[:, fc, li * 128:(li + 1) * 128], W2R[slot][:, fl, :], fc == 0, fc == 31,
                           ["W2R%d" % slot] + hTk, ["ps%d" % bank])
            for li, i in enumerate(tiles):
                SMP = (i == 16)
                bank = banks[li]
                g2b = g2S if SMP else g2P
                hv = h[:, i, half * 512:(half + 1) * 512]
                ytmp = ytmpR[li]; yk = "ytmp%d" % li
                tt(ytmp, ps[:, bank, :], g2b[:, half * 512:(half + 1) * 512], ALU.mult, ["ps%d" % bank] + g2k[SMP], [yk])
                tt(hv, hv, ytmp, ALU.add, ["h%d" % i, yk], ["h%d" % i])
        for li, i in enumerate(tiles):
            dst = yp[i * 128:(i + 1) * 128, :] if i < 16 else ys
            dma("pool", dst, h[:, i, :], ["h%d" % i], ["y%d" % i], dsem="d_y%d" % (i % 4))

    NST = len(ST_TILES)
    EARLY_KEYS = ["t2"] + ["hn2T0_%d" % li for li in range(4)] + ["W1R%d" % q for q in range(3)]
    P.add("dve", lambda e: e.memset(smallf[:, 204:205], 0.0), reads=[], writes=WIN_KEYS + EARLY_KEYS)
    mlp_norm(0)
    for s8 in range(3):
        dma("sp", W1R[s8], w1s.rearrange("(k p) n -> p k n", p=128)[:, :, s8 * 512:(s8 + 1) * 512], W1S_KEYS, ["W1R%d" % s8])
    w1_pre[0] = True
    P.fence("dve", lambda e: e.memset(smallf[:, 202:203], 0.0))

    for j in range(NST):
        mlp_ff1(j)
        f2 = P.capture(mlp_ff2, j)
        if j + 1 < NST:
            nn = P.capture(mlp_norm, j + 1)
            P.flush_positions([(f2, 0.0, 1.0), (nn, 0.02, 0.6)])
        else:
            P.flush_positions([(f2, 0.0, 1.0)])

    P.emit(nc)
    return nc


_CACHE = {}


def kernel(x_prompt, x_sample, state_gla, cache_swa_k, cache_swa_v, c_prompt, c_sample,
           w_ada, b_ada, norm1_w, norm2_w, w_in, w_gate_up, b_gate, gla_norm_w,
           q_norm_w, k_norm_w, sinks, w_out, w_ff1, w_ff2):
    f = lambda a: np.ascontiguousarray(np.asarray(a, dtype=np.float32))
    if "nc" not in _CACHE:
        _CACHE["nc"] = build()
        _CACHE["consts"] = _consts()
    nc = _CACHE["nc"]
    cf, cb = _CACHE["consts"]
    x_prompt = f(x_prompt); x_sample = f(x_sample)
    in_maps = []
    for c in range(8):
        bs = slice(16 * c, 16 * c + 16)
        in_maps.append({
            "xp": x_prompt[c], "xs": x_sample[bs].reshape(128, 1024),
            "cvec": np.concatenate([f(c_prompt)[c:c + 1], f(c_sample)[bs]], axis=0),
            "st_in": f(state_gla)[0, bs], "ck_in": f(cache_swa_k)[0, bs].reshape(16, 128, 128),
            "cv_in": f(cache_swa_v)[0, bs].reshape(16, 128, 128),
            "w_ada": f(w_ada)[0], "b_ada": f(b_ada), "n1w": f(norm1_w)[0], "n2w": f(norm2_w)[0],
            "w_in": f(w_in)[0], "wgu": f(w_gate_up)[0], "b_gate": f(b_gate), "gnw": f(gla_norm_w)[0],
            "qnw": f(q_norm_w)[0], "knw": f(k_norm_w)[0], "sinks": f(sinks)[0], "w_out": f(w_out)[0],
            "w1": f(w_ff1)[0], "w2": f(w_ff2)[0], "cst_f": cf, "cst_b": cb,
        })
    if os.environ.get("KTRACE"):
        res = run_bass_kernel_spmd(nc, in_maps, core_ids=list(range(8)), trace=True)
        print("EXEC_TIME_NS", res.exec_time_ns)
    else:
        res = run_bass_kernel_spmd(nc, in_maps, core_ids=list(range(8)))
    R = res.results
    _CACHE["last"] = R
    yp = np.stack([R[c]["yp"] for c in range(8)], 0).reshape(8, 2048, 1024)
    ys = np.concatenate([R[c]["ys"].reshape(16, 8, 1024) for c in range(8)], 0)
    gp = np.stack([R[c]["gp"] for c in range(8)], 0)[None]
    kp = np.stack([R[c]["kp"].reshape(128, 2, 64) for c in range(8)], 0)[None]
    vp = np.stack([R[c]["vp"].reshape(128, 2, 64) for c in range(8)], 0)[None]
    gs = np.concatenate([R[c]["gs"] for c in range(8)], 0)[None]
    ks = np.concatenate([R[c]["ks"].reshape(16, 128, 2, 64) for c in range(8)], 0)[None]
    vs = np.concatenate([R[c]["vs"].reshape(16, 128, 2, 64) for c in range(8)], 0)[None]
    return tuple(np.ascontiguousarray(a.astype(np.float32)) for a in (yp, ys, gp, kp, vp, gs, ks, vs))
```

```python
import contextlib
import math
import os

import numpy as np
import ml_dtypes

import concourse.bass as bass
import concourse.mybir as mybir
from concourse.bass_utils import run_bass_kernel_spmd

F32 = mybir.dt.float32
BF16 = mybir.dt.bfloat16
AF = mybir.ActivationFunctionType
ALU = mybir.AluOpType
AX = mybir.AxisListType

SAME_ENGINE_SYNC = os.environ.get("KSES", "1") == "1"
EPS = 1e-6
NT = 17
ST_TILES = [[0, 1, 2, 3], [4, 5, 6, 7], [8, 9, 10, 11], [12, 13, 14, 15, 16]]


class Prog:
    def __init__(self):
        self.ops = []
        self.last_w = {}
        self.readers = {}
        self.fence_id = None

    def fence(self, eng, fn):
        deps = set()
        last_eng = {}
        last_dsem = {}
        for j, o in enumerate(self.ops):
            if o["dma"]:
                last_dsem[o["dsem"]] = j
            last_eng[o["eng"]] = j
        deps.update(last_eng.values()); deps.update(last_dsem.values())
        if self.fence_id is not None:
            deps.add(self.fence_id)
        i = len(self.ops)
        self.ops.append(dict(eng=eng, fn=fn, deps=deps, dma=False, dsem=None, signal=False, fence=True))
        self.fence_id = i
        return i

    def add(self, eng, fn, reads=(), writes=(), dma=False, dsem=None, cost=None):
        if cost is None:
            cost = getattr(self, "next_cost", None)
        self.next_cost = None
        if cost is None:
            cost = {"pe": 0.25, "act": 0.5, "dve": 0.5, "pool": 1.0, "sp": 2.0}[eng] if not dma else 2.5
        if getattr(self, "sink", None) is not None:
            item = (eng, fn, tuple(reads), tuple(writes), dma, dsem, cost)
            if getattr(self, "group", None) is not None:
                self.group.append(item)
            else:
                self.sink.append([item])
            return -1
        i = len(self.ops)
        deps = set()
        if self.fence_id is not None:
            deps.add(self.fence_id)
        for k in reads:
            if k in self.last_w:
                deps.add(self.last_w[k])
        for k in writes:
            if k in self.last_w:
                deps.add(self.last_w[k])
            for r in self.readers.get(k, ()):
                deps.add(r)
        deps.discard(i)
        for k in writes:
            self.last_w[k] = i
            self.readers[k] = []
        for k in reads:
            if k not in writes:
                self.readers.setdefault(k, []).append(i)
        if dma and dsem is None:
            dsem = "d_" + str(writes[0])
        self.ops.append(dict(eng=eng, fn=fn, deps=deps, dma=dma, dsem=dsem, signal=False, cost=cost))
        return i

    def reschedule(self):
        ops = self.ops
        n = len(ops)
        order = []
        seg_start = 0
        bounds = [k for k, o in enumerate(ops) if o.get("fence")] + [n]
        prev = 0
        segs = []
        for b in bounds:
            if b > prev:
                segs.append((prev, b))
            if b < n:
                segs.append((b, b + 1))
            prev = b + 1
        finish = [0.0] * n
        eng_free = {}
        for (a, b) in segs:
            if b - a == 1:
                order.append(a)
                finish[a] = max([eng_free.get(ops[a]["eng"], 0.0)] + [finish[d] for d in ops[a]["deps"]]) + 0.1
                continue
            idx = list(range(a, b))
            indeg = {k: 0 for k in idx}
            succ = {k: [] for k in idx}
            for k in idx:
                for d in ops[k]["deps"]:
                    if a <= d < b:
                        indeg[k] += 1
                        succ[d].append(k)
            import heapq
            ready = [k for k in idx if indeg[k] == 0]
            t_eng = dict(eng_free)
            done = 0
            ready_set = set(ready)
            while ready_set:
                best = None
                for k in ready_set:
                    o = ops[k]
                    st = max([t_eng.get(o["eng"], 0.0)] + [finish[d] for d in o["deps"]])
                    key = (st, k)
                    if best is None or key < best[0]:
                        best = (key, k, st)
                _, k, st = best
                ready_set.discard(k)
                o = ops[k]
                if o["dma"]:
                    t_eng[o["eng"]] = st + 0.1
                    finish[k] = st + o["cost"]
                else:
                    t_eng[o["eng"]] = st + o["cost"]
                    finish[k] = st + o["cost"] + 0.15
                order.append(k)
                for s_ in succ[k]:
                    indeg[s_] -= 1
                    if indeg[s_] == 0:
                        ready_set.add(s_)
            eng_free = t_eng
        assert len(order) == n and len(set(order)) == n
        pos = {old: new for new, old in enumerate(order)}
        new_ops = [ops[k] for k in order]
        for o in new_ops:
            o["deps"] = set(pos[d] for d in o["deps"])
            for d in o["deps"]:
                pass
        for new, o in enumerate(new_ops):
            assert all(d < new for d in o["deps"]), "reschedule broke topological order"
        self.ops = new_ops
        self.est_time = max(finish) if finish else 0.0

    def capture(self, f, *a):
        self.sink = []
        self.group = None
        f(*a)
        out, self.sink = self.sink, None
        return out

    def begin_atomic(self):
        if getattr(self, "sink", None) is not None:
            self.group = []

    def end_atomic(self):
        if getattr(self, "sink", None) is not None and self.group is not None:
            self.sink.append(self.group)
            self.group = None

    def flush_positions(self, streams):
        items = []
        for si, (groups, lo, hi) in enumerate(streams):
            n = len(groups)
            for k, g in enumerate(groups):
                items.append((lo + (hi - lo) * (k + 0.5) / max(n, 1), si, k, g))
        items.sort(key=lambda t: (t[0], t[1], t[2]))
        for _, _, _, g in items:
            for it in g:
                self.add(*it[:6], cost=it[6])

    def flush_merged(self, A, B):
        ia = ib = 0
        na, nb = len(A), len(B)
        while ia < na or ib < nb:
            if ib >= nb or (ia < na and ia * nb <= ib * na):
                for it in A[ia]:
                    self.add(*it[:6], cost=it[6])
                ia += 1
            else:
                for it in B[ib]:
                    self.add(*it[:6], cost=it[6])
                ib += 1

    def emit(self, nc):
        if os.environ.get("KSCHED", "1") == "1":
            self.reschedule()
        ops = self.ops
        for o in ops:
            nd = set()
            for d in o["deps"]:
                p = ops[d]
                if (not p["dma"]) and p["eng"] == o["eng"] and not o.get("fence") and not p.get("fence"):
                    if o["eng"] == "pe" and not o["dma"]:
                        continue
                    if not SAME_ENGINE_SYNC:
                        continue
                nd.add(d)
            latest = {}
            nd2 = set()
            for d in nd:
                p = ops[d]
                if p["dma"]:
                    nd2.add(d)
                elif latest.get(p["eng"], -1) < d:
                    latest[p["eng"]] = d
            nd2.update(latest.values())
            o["deps"] = nd2
            for d in nd2:
                ops[d]["signal"] = True
        cnt = {}
        for o in ops:
            if o["dma"]:
                s = o["dsem"]
            elif o["signal"]:
                s = "e_" + o["eng"]
            else:
                o["tok"] = None
                continue
            cnt[s] = cnt.get(s, 0) + (16 if o["dma"] else 1)
            o["tok"] = (s, cnt[s])
        self.sem_counts = cnt
        with contextlib.ExitStack() as st:
            sems = {n: st.enter_context(nc.semaphore(n)) for n in sorted(cnt)}
            block = st.enter_context(nc.Block())
            final_tokens = {}
            for o in ops:
                if o["dma"]:
                    final_tokens[o["tok"][0]] = o["tok"][1]

            def run_engine(ename, eng):
                known = {}
                for o in ops:
                    if o["eng"] != ename:
                        continue
                    need = {}
                    for d in o["deps"]:
                        s, v = ops[d]["tok"]
                        if need.get(s, 0) < v:
                            need[s] = v
                    for s, v in need.items():
                        if known.get(s, 0) >= v:
                            continue
                        eng.wait_ge(sems[s], v)
                        known[s] = v
                    ins = o["fn"](eng)
                    if o["tok"] is not None:
                        ins.then_inc(sems[o["tok"][0]], 16 if o["dma"] else 1)
                if ename == "sp":
                    for s, v in final_tokens.items():
                        if known.get(s, 0) < v:
                            eng.wait_ge(sems[s], v)

            block.sync(lambda e: run_engine("sp", e))
            block.scalar(lambda e: run_engine("act", e))
            block.vector(lambda e: run_engine("dve", e))
            block.gpsimd(lambda e: run_engine("pool", e))
            block.tensor(lambda e: run_engine("pe", e))


CF_IDENT, CF_TRIP, CF_TRIS, CF_NEGP, CF_NEGS, CF_ROWSEL, CF_ROPE = 0, 128, 256, 384, 385, 401, 417
CF_W = 417 + NT * 128
CB_IDENT, CB_CMP, CB_CMS, CB_MASKB, CB_CMC, CB_ONES = 0, 128, 256, 384, 384 + 1536, 384 + 1536 + 8
CB_W = CB_ONES + 128


def _consts():
    s = np.arange(128)[:, None]
    t = np.arange(128)[None, :]
    same = (s // 8) == (t // 8)
    cf = np.zeros((128, CF_W), np.float32)
    cf[:, CF_IDENT:CF_IDENT + 128] = np.eye(128)
    cf[:, CF_TRIP:CF_TRIP + 128] = np.where(s <= t, -1.0 / 16, 0.0)
    cf[:, CF_TRIS:CF_TRIS + 128] = np.where((s <= t) & same, -1.0 / 16, 0.0)
    cf[:, CF_NEGP] = -1.0 / 16
    bsel = (np.arange(128)[:, None] // 8) == np.arange(16)[None, :]
    cf[:, CF_NEGS:CF_NEGS + 16] = np.where(bsel, -1.0 / 16, 0.0)
    cf[:, CF_ROWSEL:CF_ROWSEL + 16] = bsel.astype(np.float32)
    half = 32
    inv = np.power(np.float32(10000.0), -np.arange(half, dtype=np.float32) * np.float32(2.0) / np.float32(64)).astype(np.float32)
    for i in range(NT):
        if i < 16:
            pos = (i * 128 + np.arange(128)).astype(np.float32)
        else:
            pos = (8192 + (np.arange(128) % 8)).astype(np.float32)
        ang = (pos[:, None] * inv[None, :]).astype(np.float32)
        c = np.cos(ang).astype(np.float32)
        sn = np.sin(ang).astype(np.float32)
        base = CF_ROPE + i * 128
        cf[:, base:base + 32] = c
        cf[:, base + 32:base + 64] = c
        cf[:, base + 64:base + 96] = -sn
        cf[:, base + 96:base + 128] = sn
    cb = np.zeros((128, CB_W), np.float32)
    cb[:, CB_IDENT:CB_IDENT + 128] = np.eye(128)
    cb[:, CB_CMP:CB_CMP + 128] = (s <= t)
    cb[:, CB_CMS:CB_CMS + 128] = ((s <= t) & same)
    NEG = -30000.0
    m_prev = np.where(s > t, 0.0, NEG)
    m_cur = np.where(s <= t, 0.0, NEG)
    m_curs = np.where((s <= t) & same, 0.0, NEG)
    cb[:, CB_MASKB:CB_MASKB + 512] = np.tile(m_prev, (1, 4))
    cb[:, CB_MASKB + 512:CB_MASKB + 1024] = np.tile(m_cur, (1, 4))
    cb[:, CB_MASKB + 1024:CB_MASKB + 1536] = np.tile(m_curs, (1, 4))
    cb[:, CB_CMC:CB_CMC + 8] = (np.arange(128)[:, None] > np.arange(8)[None, :])
    cb[:, CB_ONES:CB_ONES + 128] = 1.0
    return cf, cb.astype(ml_dtypes.bfloat16)


def build():
    nc = bass.Bass("TRN2", target_bir_lowering=False)
    P = Prog()

    def din(name, shape, dt=F32):
        return nc.dram_tensor(name, shape, dt, kind="ExternalInput").ap()

    def dout(name, shape):
        return nc.dram_tensor(name, shape, F32, kind="ExternalOutput").ap()

    xp = din("xp", [2048, 1024]); xs = din("xs", [128, 1024]); cvec = din("cvec", [17, 1024])
    st_in = din("st_in", [16, 4, 64, 128]); ck_in = din("ck_in", [16, 128, 128]); cv_in = din("cv_in", [16, 128, 128])
    w_ada = din("w_ada", [1024, 6144]); b_ada = din("b_ada", [1, 6144])
    n1w = din("n1w", [1024]); n2w = din("n2w", [1024])
    w_in = din("w_in", [1024, 2320]); wgu = din("wgu", [16, 256]); b_gate = din("b_gate", [1, 256])
    gnw = din("gnw", [128]); qnw = din("qnw", [64]); knw = din("knw", [64]); sinks = din("sinks", [8])
    w_out = din("w_out", [1024, 1024]); w1 = din("w1", [1024, 4096]); w2 = din("w2", [4096, 1024])
    cst_f = din("cst_f", [128, CF_W]); cst_b = din("cst_b", [128, CB_W], BF16)

    yp = dout("yp", [2048, 1024]); ys = dout("ys", [128, 1024])
    gp = dout("gp", [4, 64, 128]); kp = dout("kp", [128, 128]); vp = dout("vp", [128, 128])
    gs = dout("gs", [16, 4, 64, 128]); ks = dout("ks", [16, 128, 128]); vs = dout("vs", [16, 128, 128])

    w1s = nc.dram_tensor("w1s", [1024, 4096], BF16, kind="Internal").ap()
    w2s = nc.dram_tensor("w2s", [4096, 1024], BF16, kind="Internal").ap()

    ARENA_W = 53200
    arena = nc.alloc_sbuf_tensor("arena", [128, ARENA_W], F32)
    ps = nc.alloc_psum_tensor("ps", [128, 8, 512], F32)
    psf = ps.rearrange("p b n -> p (b n)")

    def pbf(bank):
        return ps[:, bank, :].bitcast(BF16)

    class Carver:
        def __init__(self, lo, hi):
            self.off = lo; self.hi = hi

        def get(self, shape, dt):
            n = int(np.prod(shape[1:]))
            w = n if dt == F32 else (n + 1) // 2
            w = (w + 7) // 8 * 8
            assert self.off + w <= self.hi, ("SBUF carve overflow", shape, self.off, w, self.hi)
            v = arena[:, self.off:self.off + w]
            self.off += w
            if dt != F32:
                v = v.bitcast(dt)
            v = v[0:shape[0], 0:n]
            if len(shape) > 2:
                names = " ".join("a%d" % i for i in range(len(shape) - 1))
                v = v.rearrange("p (%s) -> p %s" % (names, names), **{"a%d" % i: shape[i + 1] for i in range(len(shape) - 1)})
            return v

    cp = Carver(0, ARENA_W)
    h = cp.get([128, NT, 1024], F32)
    g2P = cp.get([128, 1024], F32); g2S = cp.get([128, 1024], F32)
    a1 = cp.get([128, 8, 17], F32); sh1 = cp.get([128, 8, 17], F32)
    a2 = cp.get([128, 8, 17], F32); sh2 = cp.get([128, 8, 17], F32)
    cf = cp.get([128, CF_ROPE], F32)
    cb = cp.get([128, CB_W], BF16)
    ssa = cp.get([128, 64], F32)
    ssb = cp.get([128, 64], F32)
    smallf = cp.get([128, 256], F32)
    n1f = cp.get([128, 8], F32); n2f = cp.get([128, 8], F32)
    ident = cb[:, CB_IDENT:CB_IDENT + 128]
    ident32 = cf[:, CF_IDENT:CF_IDENT + 128]
    ones_row = cb[0:1, CB_ONES:CB_ONES + 128]
    X0 = cp.off

    ca = Carver(X0, ARENA_W)
    win = ca.get([128, 8, 2560], BF16)
    wout = ca.get([128, 8, 1024], BF16)
    g1P = ca.get([128, 1024], F32); g1S = ca.get([128, 1024], F32)
    ropeR = [ca.get([128, 128], F32) for _ in range(2)]
    wqk = ca.get([128, 640], F32); gnwb = ca.get([128, 128], F32)
    esink = ca.get([128, 8], F32); negc = ca.get([128, 8], F32)
    bgrow = ca.get([1, 256], BF16)
    S32 = ca.get([128, 2, 128], F32); Sbf = ca.get([128, 2, 128], BF16); tmpS = ca.get([128, 2, 128], F32)
    kTr = [ca.get([128, 128], BF16) for _ in range(2)]
    WA0 = ca.off
    tbuf = ca.get([128, 1024], BF16)
    mixR = [ca.get([128, 1024], BF16) for _ in range(2)]
    rotbR = [ca.get([128, 640], BF16) for _ in range(2)]
    vaug = [ca.get([128, 2, 65], BF16) for _ in range(3)]
    hnT = ca.get([128, 8, 128], BF16)
    e1 = ca.get([128, 256], F32)
    spb = ca.get([128, 256], F32)
    enb = ca.get([128, 256], F32); eblR = [ca.get([128, 2, 16], F32) for _ in range(2)]
    qkt = ca.get([128, 512], BF16); vb = ca.get([128, 512], BF16)
    qkT = ca.get([128, 4, 128], BF16)
    attmb = ca.get([128, 512], BF16)
    gr2 = ca.get([128, 640], F32)
    gateb = ca.get([128, 512], BF16)
    qkn = ca.get([128, 640], F32); r1 = ca.get([128, 640], F32)
    sv32 = ca.get([128, 128], F32)
    osqb = ca.get([128, 512], BF16)
    qTb = ca.get([128, 4, 128], BF16)
    E2 = [ca.get([128, 512], BF16) for _ in range(2)]
    mixT = ca.get([128, 8, 128], BF16)
    tmpf = ca.get([128, 512], F32)
    WA1 = ca.off
    Sst = ca.get([128, 2, 2, 128], F32)
    Sstb = ca.get([128, 2, 2, 128], BF16)
    Sout = ca.get([128, 2, 2, 128], F32)
    oTb = ca.get([128, 4, 128], BF16)
    km = ca.get([128, 256], BF16)
    ckb = ca.get([128, 8, 128], BF16); kcT = ca.get([128, 8, 128], BF16)
    vca = ca.get([128, 8, 2, 65], BF16)
    oTc = ca.get([128, 2, 4, 128], BF16)
    A_END = ca.off

    cpz = Carver(WA0, ARENA_W)
    NWA = 3
    wada = [cpz.get([128, 8, 512], BF16) for _ in range(NWA)]
    csb = cpz.get([17, 1024], F32); csb2 = cpz.get([17, 1024], F32); scb = cpz.get([17, 1024], BF16)
    scT = cpz.get([128, 8, 17], BF16); scP = cpz.get([128, 8, 128], BF16); scS = cpz.get([128, 8, 128], BF16)
    bfm = cpz.get([128, 48], F32)
    modraw = [cpz.get([128, 8, 17], F32) for _ in range(2)]
    glr = cpz.get([128, 8, 16], F32); glrT = cpz.get([16, 8, 128], F32); wgus = cpz.get([16, 256], F32)
    wtmp = cpz.get([128, 64], F32)
    assert cpz.off <= ARENA_W

    cbz = Carver(X0, ARENA_W)
    t2 = cbz.get([128, 1024], BF16)
    hn2T_a = cbz.get([128, 8, 512], BF16)
    W1R = [cbz.get([128, 8, 512], BF16) for _ in range(3)]
    assert cbz.off <= X0 + 10240
    hn2T_b = cbz.get([128, 8, 640], BF16)
    hn2TR = [hn2T_a, hn2T_b]
    hT = cbz.get([128, 32, 640], BF16)
    NW2 = 6
    W2R = [cbz.get([128, 4, 512], BF16) for _ in range(NW2)]
    rl = [cbz.get([128, 640], BF16) for _ in range(2)]
    ytmpR = [cbz.get([128, 512], F32) for _ in range(5)]
    assert cbz.off <= ARENA_W

    def dma(q, out, in_, reads, writes, dsem=None, nonc=False):
        nb = 1
        for d in in_.shape:
            nb *= d
        nb *= 4 if in_.dtype == F32 else 2
        P.next_cost = 2.0 + nb / 3.0e5
        if nonc:
            def f(e):
                with nc.allow_non_contiguous_dma(reason="small strided load"):
                    return e.dma_start(out=out, in_=in_)
        else:
            def f(e):
                return e.dma_start(out=out, in_=in_)
        P.add(q, f, reads=reads, writes=writes, dma=True, dsem=dsem)

    def nfree(ap):
        n = 1
        for d in ap.shape[1:]:
            n *= d
        return n

    def act(out, in_, func, reads, writes, bias=None, scale=None, accum=None):
        P.next_cost = 0.28 + nfree(out) / 1200.0
        kw = {}
        if bias is not None: kw["bias"] = bias
        if scale is not None: kw["scale"] = scale
        if accum is not None: kw["accum_out"] = accum
        P.add("act", lambda e: e.activation(out=out, in_=in_, func=func, **kw), reads=reads, writes=writes)

    def tt(out, in0, in1, op, reads, writes, eng="dve"):
        P.next_cost = (0.1 + nfree(out) / 900.0) if eng == "dve" else (0.2 + nfree(out) / 420.0)
        P.add(eng, lambda e: e.tensor_tensor(out=out, in0=in0, in1=in1, op=op), reads=reads, writes=writes)

    def ts(out, in0, s1, op0, reads, writes, s2=None, op1=None, eng="dve"):
        P.next_cost = 0.1 + nfree(out) / 1400.0
        if op1 is None:
            P.add(eng, lambda e: e.tensor_scalar(out=out, in0=in0, scalar1=s1, scalar2=None, op0=op0), reads=reads, writes=writes)
        else:
            P.add(eng, lambda e: e.tensor_scalar(out=out, in0=in0, scalar1=s1, scalar2=s2, op0=op0, op1=op1), reads=reads, writes=writes)

    def cpy(eng, out, in_, reads, writes):
        P.next_cost = (0.28 + nfree(out) / 1200.0) if eng == "act" else (0.1 + nfree(out) / 1000.0)
        if eng == "act":
            P.add("act", lambda e: e.activation(out=out, in_=in_, func=AF.Copy), reads=reads, writes=writes)
        else:
            P.add(eng, lambda e: e.tensor_copy(out=out, in_=in_), reads=reads, writes=writes)

    def mm(out, lhsT, rhs, start, stop, reads, writes):
        P.next_cost = 0.03 + max(nfree(out), 64) / 1800.0 * (4.0 if lhsT.dtype == F32 else 1.0)
        P.add("pe", lambda e: e.matmul(out, lhsT=lhsT, rhs=rhs, start=start, stop=stop), reads=reads, writes=writes)

    def tr(out, in_, idn, reads, writes):
        P.next_cost = 0.11
        P.add("pe", lambda e: e.transpose(out=out, in_=in_, identity=idn), reads=reads, writes=writes)

    DBG = {}

    def dbg(name, ap, reads):
        if not os.environ.get("KDBG"):
            return
        shp = list(ap.shape)
        d = nc.dram_tensor("dbg_" + name, shp, ap.dtype, kind="ExternalOutput").ap()
        DBG[name] = d
        dma("sp", d, ap, reads, ["dbg_" + name])

    def rstd_from_ss(ss_ap, n, out_ap, tmp_ap, D, keys_r, key_w):
        act(tmp_ap, ss_ap, AF.Ln, reads=keys_r, writes=[key_w + "_ln"], bias=EPS, scale=1.0 / D)
        act(out_ap, tmp_ap, AF.Exp, reads=[key_w + "_ln"], writes=[key_w], scale=-0.5)

    dma("sp", cf, cst_f[:, 0:CF_ROPE], [], ["cf"])
    dma("sp", cb, cst_b, [], ["cb"])
    dma("sp", csb, cvec, [], ["csb"])
    dma("sp", n1f, n1w.rearrange("(k p) -> p k", p=128), [], ["n1f"], nonc=True)
    dma("sp", n2f, n2w.rearrange("(k p) -> p k", p=128), [], ["n2f"], nonc=True)
    dma("sp", bfm, b_ada.rearrange("o (c p) -> p (o c)", p=128), [], ["bfm"], nonc=True)
    dma("sp", glr, w_in[:, 1536:1552].rearrange("(k p) r -> p k r", p=128), [], ["glr"], nonc=True)
    dma("sp", wgus, wgu, [], ["wgus"])
    dma("sp", wqk[:, 0:512].rearrange("p (h d) -> p h d", h=8), qnw.partition_broadcast(128).rearrange("p (o d) -> p o d", o=1).broadcast_to([128, 8, 64]), [], ["wqk_q"], nonc=True)
    dma("sp", wqk[:, 512:640].rearrange("p (h d) -> p h d", h=2), knw.partition_broadcast(128).rearrange("p (o d) -> p o d", o=1).broadcast_to([128, 2, 64]), [], ["wqk_k"], nonc=True)
    dma("sp", gnwb, gnw.partition_broadcast(128), [], ["gnwb"], nonc=True)
    dma("sp", esink, sinks.partition_broadcast(128), [], ["esink"], nonc=True)
    for (dst, key, blk) in ((g1P, "g1P", 2), (g1S, "g1S", 2), (g2P, "g2P", 5), (g2S, "g2S", 5)):
        for half in range(2):
            dma("sp", dst[:, half * 512:(half + 1) * 512],
                b_ada[0, blk * 1024 + half * 512: blk * 1024 + (half + 1) * 512].partition_broadcast(128), [], [key + str(half)], nonc=True)
    def load_x(i):
        src = xp[i * 128:(i + 1) * 128, :] if i < 16 else xs
        dma("sp", h[:, i, :], src, ["h%d" % (i - 2)] if i >= 2 else [], ["h%d" % i])

    load_x(0); load_x(1)
    w_ada_v = w_ada.rearrange("(k p) n -> p k n", p=128)
    w_in_v = w_in.rearrange("(k p) n -> p k n", p=128)

    def load_wada(hb):
        dma("pool", wada[hb % NWA], w_ada_v[:, :, hb * 512:(hb + 1) * 512], [], ["wada%d" % (hb % NWA)])

    load_wada(0); load_wada(1); load_wada(2)
    dma("pool", win[:, :, 0:1536], w_in_v[:, :, 0:1536], [], ["win_a"])
    dma("pool", win[:, :, 1536:2304], w_in_v[:, :, 1552:2320], [], ["win_b"])
    dma("pool", bgrow, b_gate, [], ["bgrow"])

    P.add("dve", lambda e: e.memset(ssa, 0.0), writes=["ssa"])
    P.add("dve", lambda e: e.memset(ssb, 0.0), writes=["ssb"])

    act(csb2, csb, AF.Exp, ["csb"], ["csb2"], scale=-1.0)
    act(csb2, csb2, AF.Ln, ["csb2"], ["csb2"], bias=1.0, scale=1.0)
    act(csb2, csb2, AF.Exp, ["csb2"], ["csb2"], scale=-1.0)
    tt(scb, csb, csb2, ALU.mult, ["csb", "csb2"], ["scb"])
    pT = pbf(5)
    for k in range(8):
        tr(pT[:, k * 32:k * 32 + 17], scb[:, k * 128:(k + 1) * 128], ident[0:17, 0:17], ["scb", "cb"], ["ps5"])
    cpy("dve", scT, pT[:, 0:256].rearrange("p (k c) -> p k c", k=8)[:, :, 0:17], ["ps5"], ["scT"])
    cpy("dve", scP, scT[:, :, 0:1].broadcast_to([128, 8, 128]), ["scT"], ["scP"])
    cpy("dve", scS.rearrange("p k (b t) -> p k b t", t=8),
        scT[:, :, 1:17].rearrange("p k (b o) -> p k b o", o=1).broadcast_to([128, 8, 16, 8]), ["scT"], ["scS"])

    for k in range(8):
        tr(psf[0:16, 3072 + k * 128:3072 + (k + 1) * 128], glr[:, k, :], ident32, ["glr", "cf"], ["ps6", "ps7"])
    cpy("act", glrT.rearrange("p k d -> p (k d)"), psf[0:16, 3072:4096], ["ps6", "ps7"], ["glrT"])
    for k in range(8):
        mm(ps[:, k % 2, 0:256], glrT[:, k, :], wgus, True, True, ["glrT", "wgus"], ["ps%d" % (k % 2)])
        cpy("dve", win[:, k, 2304:2560], ps[:, k % 2, 0:256], ["ps%d" % (k % 2)], ["win_g%d" % k])

    def fm_half(hb, dst, dkey):
        blk, hf = hb // 2, hb % 2
        sl = wada[hb % NWA]; skey = "wada%d" % (hb % NWA)
        for c4 in range(4):
            cc = hf * 4 + c4
            for k in range(8):
                mm(ps[:, 4, cc * 17:(cc + 1) * 17], sl[:, k, c4 * 128:(c4 + 1) * 128], scT[:, k, :], k == 0, k == 7, [skey, "scT"], ["ps4"])
        tt(dst[:, hf * 4:(hf + 1) * 4, :], ps[:, 4, hf * 68:(hf + 1) * 68].rearrange("p (k c) -> p k c", k=4),
           bfm[:, blk * 8 + hf * 4: blk * 8 + (hf + 1) * 4].rearrange("p (k o) -> p k o", o=1).broadcast_to([128, 4, 17]), ALU.add,
           ["ps4", "bfm"], [dkey + str(hf)])

    def tm_half(hb, dstP, dstS, keyP, keyS):
        blk, hf = hb // 2, hb % 2
        sl = wada[hb % NWA]; skey = "wada%d" % (hb % NWA)
        for (lhs, lk, dst, dk, bank) in ((scP, "scP", dstP, keyP, 0), (scS, "scS", dstS, keyS, 1)):
            for k in range(8):
                mm(ps[:, bank, :], lhs[:, k, :], sl[:, k, :], k == 0, k == 7, [skey, lk], ["ps%d" % bank])
            dv = dst[:, hf * 512:(hf + 1) * 512]
            tt(dv, ps[:, bank, :], dv, ALU.add, ["ps%d" % bank, dk + str(hf)], [dk + str(hf)])

    for hb in range(12):
        blk = hb // 2
        if blk == 0:
            fm_half(hb, sh1, "sh1_")
        elif blk == 1:
            fm_half(hb, modraw[0], "mr0_")
        elif blk == 2:
            tm_half(hb, g1P, g1S, "g1P", "g1S")
        elif blk == 3:
            fm_half(hb, sh2, "sh2_")
        elif blk == 4:
            fm_half(hb, modraw[1], "mr1_")
        else:
            tm_half(hb, g2P, g2S, "g2P", "g2S")
        if hb + NWA < 12:
            load_wada(hb + NWA)
    for (mr, mk, nf, nk, dst, dk) in ((modraw[0], "mr0_", n1f, "n1f", a1, "a1"), (modraw[1], "mr1_", n2f, "n2f", a2, "a2")):
        ts(mr, mr, 1.0, ALU.add, [mk + "0", mk + "1"], [mk + "0", mk + "1"])
        tt(dst, mr, nf.rearrange("p (k o) -> p k o", o=1).broadcast_to([128, 8, 17]), ALU.mult, [mk + "0", mk + "1", nk], [dk])
    SH1K = ["sh1_0", "sh1_1"]; SH2K = ["sh2_0", "sh2_1"]

    tt(wtmp[:, 0:64], wqk[:, 0:64], wqk[:, 0:64], ALU.mult, ["wqk_q"], ["wtmp"])
    P.add("dve", lambda e: e.tensor_reduce(out=negc[:, 1:2], in_=wtmp[:, 0:64], axis=AX.X, op=ALU.max), reads=["wtmp"], writes=["negc1"])
    tt(wtmp[:, 0:64], wqk[:, 512:576], wqk[:, 512:576], ALU.mult, ["wqk_k", "negc1"], ["wtmp"])
    P.add("dve", lambda e: e.tensor_reduce(out=negc[:, 2:3], in_=wtmp[:, 0:64], axis=AX.X, op=ALU.max), reads=["wtmp"], writes=["negc2"])
    tt(negc[:, 3:4], negc[:, 1:2], negc[:, 2:3], ALU.mult, ["negc1", "negc2"], ["negc3"])
    act(negc[:, 4:5], negc[:, 3:4], AF.Ln, ["negc3"], ["negc4"])
    act(negc[:, 5:6], negc[:, 4:5], AF.Exp, ["negc4"], ["negc5"], scale=0.5)
    ts(negc[:, 0:1], negc[:, 5:6], -8.0, ALU.mult, ["negc5"], ["negc"])
    act(esink, esink, AF.Exp, ["esink", "negc"], ["esink"], bias=negc[:, 0:1], scale=1.0)

    P.fence("dve", lambda e: e.memset(smallf[:, 200:201], 0.0))
    dma("pool", wout, w_out.rearrange("(k p) n -> p k n", p=128), [], ["wout"])
    def scratch_dma(q8):
        if q8 < 4:
            q = q8
            dma("pool", w1s[q * 256:(q + 1) * 256, :], w1[q * 256:(q + 1) * 256, :], ["wscr"], ["w1s%d" % q, "wscr"], dsem="d_w1s%d" % q)
        else:
            q = q8 - 4
            dma("pool", w2s[q * 1024:(q + 1) * 1024, :], w2[q * 1024:(q + 1) * 1024, :], ["wscr"], ["w2s%d" % q, "wscr"], dsem="d_w2s%d" % q)

    W1S_KEYS = ["w1s%d" % q for q in range(4)]
    W2S_KEYS = ["w2s%d" % q for q in range(4)]

    P.add("dve", lambda e: e.memset(S32, 0.0), writes=["S32"])
    for q in range(3):
        P.add("dve", lambda e, q=q: e.memset(vaug[q], 1.0), writes=["vaug%d" % q])
    P.add("dve", lambda e: e.memset(vca, 1.0), writes=["vca"])
    dma("pool", ks[:, 0:120, :], ck_in[:, 8:128, :], [], ["ks_past"])
    dma("pool", vs[:, 0:120, :], cv_in[:, 8:128, :], [], ["vs_past"])

    WIN_KEYS = ["win_a", "win_b"] + ["win_g%d" % k for k in range(8)]

    def mixer_s1(i, part):
        SMP = (i == 16)
        par = i % 2
        hi = h[:, i, :]
        hk = "h%d" % i
        rp = ropeR[par]; rk = "rope%d" % par
        mix = mixR[par]; mixk = "mix%d" % par
        rotb = rotbR[par]; rotk = "rotb%d" % par
        vq = i % 3
        ebl = eblR[par]; eblk = "ebl%d" % par
        OA, OB = (3, 4) if SMP else (0, 2)
        if part == "Nn":
            dma("sp", rp, cst_f[:, CF_ROPE + i * 128: CF_ROPE + (i + 1) * 128], [], [rk])
            if i + 2 < NT:
                load_x(i + 2)
            if 2 <= i < 10:
                scratch_dma(i - 2)

        def modb(m):
            if SMP:
                return m[:, :, 1:17].rearrange("p k (b o) -> p k b o", o=1).broadcast_to([128, 8, 16, 8])
            return m[:, :, 0:1].broadcast_to([128, 8, 128])

        def fmv(t):
            if SMP:
                return t.rearrange("p k (b t) -> p k b t", t=8)
            return t

        pT = pbf(5)
        if part == "Nn":
            tb = tbuf
            act(tb, hi, AF.Square, [hk, "ssa"], ["tbuf", "ssa%d" % i], accum=ssa[:, i:i + 1])
            rstd_from_ss(ssa[:, i:i + 1], 1, smallf[:, 0:1], smallf[:, 1:2], 1024.0, ["ssa%d" % i], "rs1")
            ts(tb, hi, smallf[:, 0:1], ALU.mult, [hk, "rs1"], ["tbuf"])
            pT = pbf(5)
            P.begin_atomic()
            for k in range(8):
                tr(pT[:, k * 128:(k + 1) * 128], tb[:, k * 128:(k + 1) * 128], ident, ["tbuf", "cb"], ["ps5"])
            pTv = pT.rearrange("p (k t) -> p k t", k=8)
            tt(fmv(hnT), fmv(pTv), modb(a1), ALU.mult, ["ps5", "a1"], ["hnT"])
            P.end_atomic()
            tt(fmv(hnT), fmv(hnT), modb(sh1), ALU.add, ["hnT"] + SH1K, ["hnT"])
            return
        def proj_bank(bank):
            for k in range(8):
                mm(ps[:, bank, :], hnT[:, k, :], win[:, k, bank * 512:(bank + 1) * 512], k == 0, k == 7,
                   ["hnT"] + WIN_KEYS, ["ps%d" % bank])

        tri = cf[:, CF_TRIS:CF_TRIS + 128] if SMP else cf[:, CF_TRIP:CF_TRIP + 128]
        NB = 16 if SMP else 1
        negsel = cf[:, CF_NEGS:CF_NEGS + 16] if SMP else cf[:, CF_NEGP:CF_NEGP + 1]
        eb = e1
        if part == "Npa":
            for k in range(8):
                mm(ps[:, 4, 256:512], hnT[:, k, :], win[:, k, 2304:2560], k == 0, False, ["hnT"] + WIN_KEYS, ["ps4"])
            mm(ps[:, 4, 256:512], ones_row[0:1, 0:128], bgrow[0:1, :], False, True, ["bgrow", "cb"], ["ps4"])
            act(e1, ps[:, 4, 256:512], AF.Exp, ["ps4"], ["e1"], scale=-1.0)
            act(spb, e1, AF.Ln, ["e1"], ["spb"], bias=1.0, scale=1.0)
            proj_bank(3)
            mm(ps[:, 4, 0:256], tri, spb, True, True, ["spb", "cf"], ["ps4"])
            for j in range(2):
                mm(ps[:, 4, 256 + j * NB:256 + (j + 1) * NB], spb[:, j * 128:(j + 1) * 128], negsel, True, True, ["spb", "cf"], ["ps4"])
            act(eb, ps[:, 4, 0:256], AF.Exp, ["ps4"], ["e1"], bias=math.log(0.125), scale=1.0)
            act(enb, ps[:, 4, 0:256], AF.Exp, ["ps4"], ["enb"], scale=-1.0)
            act(ebl[:, :, 0:NB], ps[:, 4, 256:256 + 2 * NB].rearrange("p (j b) -> p j b", j=2), AF.Exp, ["ps4"], [eblk])
            for k in range(8):
                mm(ps[:, 4, 0:256], hnT[:, k, :], win[:, k, 2048:2304], k == 0, k == 7, ["hnT"] + WIN_KEYS, ["ps4"])
            return
        if part == "Npb":
            proj_bank(0)
            proj_bank(1)
            proj_bank(2)
            tt(qkt[:, 0:256], ps[:, 0, 0:256], eb, ALU.mult, ["ps0", "e1"], ["qkt_q"])
            tt(qkt[:, 256:512], ps[:, 0, 256:512], enb, ALU.mult, ["ps0", "enb"], ["qkt_k"])
            cpy("act", vb, ps[:, 1, :], ["ps1"], ["vb"])
            P.begin_atomic()
            for c in range(4):
                tr(pT[:, c * 128:(c + 1) * 128], qkt[:, c * 128:(c + 1) * 128], ident, ["qkt_q", "qkt_k", "cb"], ["ps5"])
            cpy("act", qkT, pT[:, 0:512].rearrange("p (c t) -> p c t", c=4), ["ps5"], ["qkT"])
            P.end_atomic()
            return
        qkps = psf[:, 1536:2176]
        sq2 = r1
        v10 = lambda t_: t_.rearrange("p (h d) -> p h d", h=10)
        gate = gateb
        rot32 = r1
        if part == "B1g":
            act(gate, ps[:, 2, :], AF.Exp, ["ps2"], ["gateb"], scale=-1.0)
            act(gate, gate, AF.Ln, ["gateb"], ["gateb"], bias=1.0, scale=1.0)
            act(gate, gate, AF.Exp, ["gateb"], ["gateb"], scale=-1.0)
            tt(gate, ps[:, 2, :], gate, ALU.mult, ["ps2", "gateb"], ["gateb"])
            tt(gate.rearrange("p (h d) -> p h d", h=4), gate.rearrange("p (h d) -> p h d", h=4),
               gnwb.rearrange("p (o d) -> p o d", o=1).broadcast_to([128, 4, 128]), ALU.mult, ["gateb", "gnwb"], ["gateb"])
            return
        if part == "B1q":
            act(sq2, qkps, AF.Square, ["ps3", "ps4"], ["r1"])
            P.add("dve", lambda e: e.tensor_reduce(out=smallf[:, 24:34], in_=sq2.rearrange("p (h d) -> p h d", h=10), axis=AX.X, op=ALU.add),
                  reads=["r1"], writes=["ssq"])
            rstd_from_ss(smallf[:, 24:34], 10, smallf[:, 40:50], smallf[:, 56:66], 64.0, ["ssq"], "rqk")
            tt(v10(qkn), v10(qkps), smallf[:, 40:50].rearrange("p (h o) -> p h o", o=1).broadcast_to([128, 10, 64]), ALU.mult,
               ["ps3", "ps4", "rqk"], ["qkn"])
            cpy("dve", vaug[vq][:, :, 0:64], ps[:, 4, 128:256].rearrange("p (g d) -> p g d", g=2), ["ps4"], ["vaug%d" % vq])
            if i == 15 or SMP:
                cpy("act", sv32, ps[:, 4, 128:256], ["ps4"], ["sv32"])
            return
        if part == "B2":
            PE_ = os.environ.get("KPOOL", "1") == "1" and "pool" or "dve"
            tt(qkn, qkn, wqk, ALU.mult, ["qkn", "wqk_q", "wqk_k"], ["qkn"], eng=PE_)
            cosb = rp[:, 0:64].rearrange("p (o d) -> p o d", o=1).broadcast_to([128, 10, 64])
            sin_lo = rp[:, 64:96].rearrange("p (o d) -> p o d", o=1).broadcast_to([128, 10, 32])
            sin_hi = rp[:, 96:128].rearrange("p (o d) -> p o d", o=1).broadcast_to([128, 10, 32])
            r2 = gr2
            tt(v10(r1), v10(qkn), cosb, ALU.mult, ["qkn", rk], ["r1"], eng=PE_)
            tt(v10(r2)[:, :, 0:32], v10(qkn)[:, :, 32:64], sin_lo, ALU.mult, ["qkn", rk], ["gr2"], eng=PE_)
            tt(v10(r2)[:, :, 32:64], v10(qkn)[:, :, 0:32], sin_hi, ALU.mult, ["qkn", rk], ["gr2"], eng=PE_)
            tt(rot32, r1, r2, ALU.add, ["r1", "gr2"], ["r1"], eng=PE_)
            cpy("act", rotb[:, 0:512].rearrange("p (j g d) -> p g j d", j=4, g=2), rot32[:, 0:512].rearrange("p (g j d) -> p g j d", g=2, j=4),
                ["r1"], [rotk])
            cpy("act", rotb[:, 512:640], rot32[:, 512:640], ["r1"], [rotk])
            if i == 15:
                dma("pool", kp, rot32[:, 512:640], ["r1"], ["kp"])
                dma("pool", vp, sv32, ["sv32"], ["vp"])
            elif SMP:
                dma("pool", ks[:, 120:128, :], rot32[:, 512:640], ["r1"], ["ks_new"])
                dma("pool", vs[:, 120:128, :], sv32[:, :], ["sv32"], ["vs_new"])
            return

        for hh in range(4):
            p_, j = hh % 2, hh // 2
            ob_ = OA if p_ == 0 else OB
            mm(ps[:, ob_, j * 128:(j + 1) * 128], qkT[64 * p_:64 * p_ + 64, 2 + j, :], qkT[64 * p_:64 * p_ + 64, j, :], True, True,
               ["qkT"], ["ps%d" % ob_])
        cm = cb[:, CB_CMS:CB_CMS + 128] if SMP else cb[:, CB_CMP:CB_CMP + 128]
        attm = attmb.rearrange("p (h t) -> p h t", h=4)
        for q_ in range(2):
            ob_ = OA if q_ == 0 else OB
            tt(attm.rearrange("p (j q) t -> p q j t", q=2)[:, q_], ps[:, ob_, 0:256].rearrange("p (j t) -> p j t", j=2),
               cm.rearrange("p (o t) -> p o t", o=1).broadcast_to([128, 2, 128]),
               ALU.mult, ["ps%d" % ob_, "cb"], ["attm%d" % q_])

        if SMP:
            for grp in range(8):
                for p_ in range(2):
                    src = st_in[2 * grp:2 * grp + 2].rearrange("b (j p) d v -> p d b j v", p=2)[p_]
                    dma("sp", Sst[64 * p_:64 * p_ + 64], src, [], ["Sst_%d" % p_])
                cpy("act", Sstb, Sst, ["Sst_0", "Sst_1"], ["Sstb"])
                for bl in range(2):
                    b = 2 * grp + bl
                    for hh in range(4):
                        p_, j = hh % 2, hh // 2
                        ob = 0 if p_ == 0 else 2
                        mm(ps[:, ob, j * 128 + b * 8: j * 128 + b * 8 + 8], Sstb[64 * p_:64 * p_ + 64, bl, j, :],
                           qkT[64 * p_:64 * p_ + 64, j, b * 8:(b + 1) * 8], True, True, ["Sstb", "qkT"], ["ps%d" % ob])
                for bl in range(2):
                    b = 2 * grp + bl
                    ts(km, qkt[:, 256:512], cf[:, CF_ROWSEL + b:CF_ROWSEL + b + 1], ALU.mult, ["qkt_k", "cf"], ["km"])
                    for j in range(2):
                        mm(ps[:, 1, j * 256:(j + 1) * 256], km[:, j * 128:(j + 1) * 128], vb[:, j * 256:(j + 1) * 256], True, True,
                           ["km", "vb"], ["ps1"])
                    dsv = ps[:, 1, :].rearrange("p (j c) -> p j c", j=2)
                    for p_ in range(2):
                        rs = slice(64 * p_, 64 * p_ + 64)
                        tt(tmpS[rs], dsv[rs, :, 128 * p_:128 * p_ + 128], Sst[rs, bl, :, :], ALU.add,
                           ["ps1", "Sst_0", "Sst_1"], ["tmpS%d" % p_])
                        tt(Sout[rs, bl, :, :], tmpS[rs], ebl[rs, :, b:b + 1].broadcast_to([64, 2, 128]), ALU.mult,
                           ["tmpS%d" % p_, eblk], ["Sout_%d_%d" % (bl, p_)])
                for p_ in range(2):
                    dst = gs[2 * grp:2 * grp + 2].rearrange("b (j p) d v -> p d b j v", p=2)[p_]
                    dma("pool", dst, Sout[64 * p_:64 * p_ + 64], ["Sout_%d_%d" % (bl, p_) for bl in range(2)],
                        ["gs_%d_%d" % (grp, p_)], dsem="d_Sout_%d" % p_)
            for q_ in range(2):
                ob = 0 if q_ == 0 else 2
                cpy("act", oTb.rearrange("p (j q) t -> p q j t", q=2)[:, q_], ps[:, ob, 0:256].rearrange("p (j t) -> p j t", j=2),
                    ["ps%d" % ob], ["oTb%d" % q_])

        for hh in range(4):
            p_, j = hh % 2, hh // 2
            last = (i == 0)
            ob_ = OA if p_ == 0 else OB
            ok_ = "ps%d" % ob_
            mm(ps[:, ob_, j * 128:(j + 1) * 128], attm[:, hh, :], vb[:, hh * 128:(hh + 1) * 128], True, last, ["attm0", "attm1", "vb"], [ok_])
            if SMP:
                mm(ps[:, ob_, j * 128:(j + 1) * 128], oTb[:, hh, :], ident, False, True, ["oTb0", "oTb1", "cb"], [ok_])
            elif i > 0:
                mm(ps[:, ob_, j * 128:(j + 1) * 128], qkT[64 * p_:64 * p_ + 64, j, :], Sbf[64 * p_:64 * p_ + 64, j, :], False, True,
                   ["qkT", "Sbf"], [ok_])
        if not SMP:
            for j in range(2):
                mm(ps[:, 1, j * 256:(j + 1) * 256], qkt[:, 256 + j * 128:256 + (j + 1) * 128], vb[:, j * 256:(j + 1) * 256], True, True,
                   ["qkt_k", "vb"], ["ps1"])
            dsv = ps[:, 1, :].rearrange("p (j c) -> p j c", j=2)
            for p_ in range(2):
                rs = slice(64 * p_, 64 * p_ + 64)
                tt(tmpS[rs], dsv[rs, :, 128 * p_:128 * p_ + 128], S32[rs], ALU.add, ["ps1", "S32"], ["tmpS%d" % p_])
                tt(S32[rs], tmpS[rs], ebl[rs, :, 0:1].broadcast_to([64, 2, 128]), ALU.mult, ["tmpS%d" % p_, eblk], ["S32"])
            cpy("act", Sbf, S32, ["S32"], ["Sbf"])
            if i == 15:
                for p_ in range(2):
                    dma("pool", gp.rearrange("(j p) d v -> p d j v", p=2)[p_], S32[64 * p_:64 * p_ + 64], ["S32"], ["gp%d" % p_])

        act(osqb[:, 0:256], ps[:, OA, 0:256], AF.Square, ["ps%d" % OA], ["osq0"])
        act(osqb[:, 256:512], ps[:, OB, 0:256], AF.Square, ["ps%d" % OB], ["osq1"])
        P.add("dve", lambda e: e.tensor_reduce(out=smallf[:, 8:12], in_=osqb.rearrange("p (h d) -> p h d", h=4), axis=AX.X, op=ALU.add),
              reads=["osq0", "osq1"], writes=["ssg"])
        rstd_from_ss(smallf[:, 8:12], 4, smallf[:, 12:16], smallf[:, 16:20], 128.0, ["ssg"], "rsg")
        for hh in range(4):
            p_, j = hh % 2, hh // 2
            ri = 12 + p_ * 2 + j
            P.add("dve", lambda e, hh=hh, p_=p_, j=j, ri=ri: e.scalar_tensor_tensor(
                out=mix[:, hh * 128:(hh + 1) * 128], in0=ps[:, (OA if p_ == 0 else OB), j * 128:(j + 1) * 128],
                scalar=smallf[:, ri:ri + 1], in1=gate[:, hh * 128:(hh + 1) * 128], op0=ALU.mult, op1=ALU.mult),
                  reads=["ps%d" % OA, "ps%d" % OB, "rsg", "gateb"], writes=[mixk + "g"])


    def mixer_s2(i):
        SMP = (i == 16)
        par = i % 2
        hk = "h%d" % i
        mix = mixR[par]; mixk = "mix%d" % par
        rotb = rotbR[par]; rotk = "rotb%d" % par
        vq = i % 3; vpq = (i - 1) % 3
        cur = i % 2; prv = 1 - cur
        pT = pbf(5)
        qT = qTb
        P.begin_atomic()
        for c in range(5):
            tr(pT[:, c * 128:(c + 1) * 128], rotb[:, c * 128:(c + 1) * 128], ident, [rotk, "cb"], ["ps5"])
        cpy("act", qT, pT[:, 0:512].rearrange("p (c t) -> p c t", c=4), ["ps5"], ["qT"])
        cpy("act", kTr[cur], pT[:, 512:640], ["ps5"], ["kTr%d" % cur])
        P.end_atomic()
        blocks = []
        if (not SMP) and i > 0:
            blocks.append((kTr[prv], "kTr%d" % prv, vaug[vpq], "vaug%d" % vpq, CB_MASKB, 0))
        blocks.append((kTr[cur], "kTr%d" % cur, vaug[vq], "vaug%d" % vq, CB_MASKB + (1024 if SMP else 512), 1))
        osw = ps[:, 6:8, 0:260].rearrange("p g (j e) -> p g j e", e=65)
        Eg = [[None, None], [None, None]]
        if SMP:
            Esm = [(E2[0], "E2_0"), (mixT.rearrange("p k t -> p (k t)")[:, 0:512], "mixT")]
        for g in range(2):
            for (kt, ktk, va, vak, mo, bi) in blocks:
                bank = 6 + bi
                mm(ps[:, bank, :], kt[64 * g:64 * g + 64, :], qT[64 * g:64 * g + 64, :, :], True, False, [ktk, "qT"], ["ps%d" % bank])
                mm(ps[:, bank, :], ident, cb[:, mo:mo + 512], False, True, ["cb"], ["ps%d" % bank])
                dstE = E2[bi] if g == 0 else (mixT.rearrange("p k t -> p (k t)")[:, bi * 512:(bi + 1) * 512])
                dk = ("E2_%d" % bi) if g == 0 else "mixT"
                act(dstE, ps[:, bank, :], AF.Exp, ["ps%d" % bank, "negc"], [dk], bias=negc[:, 0:1], scale=0.125)
                Eg[g][bi] = (dstE, dk)
        if SMP:
            for r in range(2):
                Ecr, Eck = Esm[r]
                dma("pool", ckb, ck_in[8 * r:8 * r + 8].rearrange("b c e -> c b e"), [], ["ckb"])
                for g in range(2):
                    dma("pool", vca[:, :, g, 0:64], cv_in[8 * r:8 * r + 8, :, g * 64:(g + 1) * 64].rearrange("b c d -> c b d"), [], ["vca"],
                        dsem="d_vca%d" % g)
                P.begin_atomic()
                for bb in range(8):
                    tr(pT[:, bb * 128:(bb + 1) * 128], ckb[:, bb, :], ident, ["ckb", "cb"], ["ps5"])
                cpy("act", kcT, pT.rearrange("p (b c) -> p b c", b=8), ["ps5"], ["kcT"])
                P.end_atomic()
                for bb in range(8):
                    b = r * 8 + bb
                    for g in range(2):
                        mm(ps[:, 6 + g, bb * 32: bb * 32 + 32], kcT[64 * g:64 * g + 64, bb, :],
                           qT[64 * g:64 * g + 64, :, b * 8:(b + 1) * 8], True, True, ["kcT", "qT"], ["ps%d" % (6 + g)])
                for g in range(2):
                    act(Ecr[:, g * 256:(g + 1) * 256], ps[:, 6 + g, 0:256], AF.Exp, ["ps%d" % (6 + g), "negc"], [Eck], bias=negc[:, 0:1], scale=0.125)
                tt(Ecr.rearrange("p (x t) -> p x t", t=8), Ecr.rearrange("p (x t) -> p x t", t=8),
                   cb[:, CB_CMC:CB_CMC + 8].rearrange("p (o t) -> p o t", o=1).broadcast_to([128, 64, 8]), ALU.mult,
                   [Eck, "cb"], [Eck])
                for bb in range(8):
                    for g in range(2):
                        mm(ps[0:65, 6 + g, bb * 32: bb * 32 + 32], vca[:, bb, g, :],
                           Ecr[:, g * 256 + bb * 32: g * 256 + bb * 32 + 32], True, True, ["vca", Eck], ["ps%d" % (6 + g)])
                for g in range(2):
                    cpy("act", oTc[0:65, g, :, r * 64:(r + 1) * 64].rearrange("p j (b t) -> p b j t", t=8),
                        ps[0:65, 6 + g, 0:256].rearrange("p (b j t) -> p b j t", b=8, j=4),
                        ["ps%d" % (6 + g)], ["oTc%d_%d" % (r, g)])
        for g in range(2):
            for j in range(4):
                out = ps[:, 6 + g, j * 65:(j + 1) * 65]
                n = len(blocks) + (1 if SMP else 0)
                c_ = 0
                for (kt, ktk, va, vak, mo, bi) in blocks:
                    dE, dk = Eg[g][bi]
                    mm(out, dE[:, j * 128:(j + 1) * 128], va[:, g, :], c_ == 0, c_ == n - 1, [dk, vak], ["ps%d" % (6 + g)])
                    c_ += 1
                if SMP:
                    mm(out, oTc[0:65, g, j, :], ident[0:65, 0:65], False, True, ["oTc0_0", "oTc0_1", "oTc1_0", "oTc1_1", "cb"], ["ps%d" % (6 + g)])
        den = smallf[:, 72:80].rearrange("p (g j) -> p g j", g=2)
        rden = smallf[:, 80:88].rearrange("p (g j) -> p g j", g=2)
        tt(den, osw[:, :, :, 64], esink.rearrange("p (g j) -> p g j", g=2), ALU.add, ["ps6", "ps7", "esink"], ["den"])
        P.add("dve", lambda e: e.reciprocal(out=rden, in_=den), reads=["den"], writes=["rden"])
        tt(mix[:, 512:1024].rearrange("p (g j d) -> p g j d", g=2, j=4), osw[:, :, :, 0:64],
           smallf[:, 80:88].rearrange("p (g j o) -> p g j o", g=2, o=1).broadcast_to([128, 2, 4, 64]), ALU.mult,
           ["ps6", "ps7", "rden"], [mixk + "s"])

        P.begin_atomic()
        for e_ in range(8):
            tr(pT[:, e_ * 128:(e_ + 1) * 128], mix[:, e_ * 128:(e_ + 1) * 128], ident, [mixk + "g", mixk + "s", "cb"], ["ps5"])
        cpy("act", mixT, pT.rearrange("p (k t) -> p k t", k=8), ["ps5"], ["mixT"])
        P.end_atomic()
        g1b = g1S if SMP else g1P
        for half in range(2):
            for e_ in range(8):
                mm(ps[:, 6 + half, :], mixT[:, e_, :], wout[:, e_, half * 512:(half + 1) * 512], e_ == 0, e_ == 7, ["mixT", "wout"], ["ps%d" % (6 + half)])
            g1k = ("g1S%d" if SMP else "g1P%d") % half
            hv = h[:, i, half * 512:(half + 1) * 512]
            tt(tmpf, ps[:, 6 + half, :], g1b[:, half * 512:(half + 1) * 512], ALU.mult, ["ps%d" % (6 + half), g1k], ["tmpf"])
            tt(hv, hv, tmpf, ALU.add, [hk, "tmpf"], [hk])

    KSTOP = os.environ.get("KSTOP", "")
    if KSTOP == "pro":
        P.emit(nc)
        return nc
    order = list(range(NT))
    if KSTOP.startswith("a") and len(KSTOP) > 1:
        order = order[:int(KSTOP[1:])]
    if KSTOP.startswith("s"):
        order = [16]
    PIPE = os.environ.get("KPIPE", "1") == "1"
    prev_s2 = None
    for i in order:
        if i == order[0]:
            for pn in ("Nn", "Npa", "B1q"):
                P.flush_positions([(P.capture(mixer_s1, i, pn), 0.0, 1.0)])
        sNb = P.capture(mixer_s1, i, "Npb")
        sBg = P.capture(mixer_s1, i, "B1g")
        sA2 = P.capture(mixer_s1, i, "A2")
        sB2 = P.capture(mixer_s1, i, "B2")
        nxt = order[order.index(i) + 1] if order.index(i) + 1 < len(order) else None
        if nxt is not None:
            for pn in ("Nn", "Npa", "B1q"):
                sB2 = sB2 + P.capture(mixer_s1, nxt, pn)
        X = sNb + sBg + sA2
        if PIPE:
            f1 = (len(sNb) + len(sBg) + 0.5) / len(X)
            streams = [(X, 0.0, 1.0), (sB2, f1, 1.0)]
            if prev_s2 is not None:
                streams.append((prev_s2, 0.0, 1.0))
            P.flush_positions(streams)
        else:
            if prev_s2 is not None:
                P.flush_positions([(prev_s2, 0.0, 1.0)])
            P.flush_positions([(X, 0.0, 1.0)]); P.flush_positions([(sB2, 0.0, 1.0)])
        prev_s2 = P.capture(mixer_s2, i)
    P.flush_positions([(prev_s2, 0.0, 1.0)])
    if KSTOP.startswith("a") or KSTOP.startswith("s"):
        dbg("h", h, ["h%d" % i for i in range(NT)])
        P.emit(nc)
        return nc

    w1_cnt = [0]; w2_cnt = [0]; w1_pre = [False]

    def mlp_norm(j):
        tiles = ST_TILES[j]
        hn2 = hn2TR[j % 2]
        for li, i in enumerate(tiles):
            SMP = (i == 16)
            hi = h[:, i, :]; hk = "h%d" % i
            act(t2, hi, AF.Square, [hk, "ssb"], ["t2", "ssb%d" % i], accum=ssb[:, i:i + 1])
            rstd_from_ss(ssb[:, i:i + 1], 1, smallf[:, 100:101], smallf[:, 101:102], 1024.0, ["ssb%d" % i], "rs2")
            ts(t2, hi, smallf[:, 100:101], ALU.mult, [hk, "rs2"], ["t2"])
            pT = pbf(7)
            P.begin_atomic()
            for k in range(8):
                tr(pT[:, k * 128:(k + 1) * 128], t2[:, k * 128:(k + 1) * 128], ident, ["t2", "cb"], ["ps7"])
            pTv = pT.rearrange("p (k t) -> p k t", k=8)
            dstv = hn2[:, :, li * 128:(li + 1) * 128]
            if SMP:
                ma = a2[:, :, 1:17].rearrange("p k (b o) -> p k b o", o=1).broadcast_to([128, 8, 16, 8])
                ms = sh2[:, :, 1:17].rearrange("p k (b o) -> p k b o", o=1).broadcast_to([128, 8, 16, 8])
                dv = dstv.rearrange("p k (b t) -> p k b t", t=8); sv_ = pTv.rearrange("p k (b t) -> p k b t", t=8)
            else:
                ma = a2[:, :, 0:1].broadcast_to([128, 8, 128]); ms = sh2[:, :, 0:1].broadcast_to([128, 8, 128])
                dv = dstv; sv_ = pTv
            tt(dv, sv_, ma, ALU.mult, ["ps7", "a2"], ["hn2T%d_%d" % (j % 2, li)])
            P.end_atomic()
            tt(dv, dv, ms, ALU.add, ["hn2T%d_%d" % (j % 2, li)] + SH2K, ["hn2T%d_%d" % (j % 2, li)])

    def mlp_ff1(j):
        tiles = ST_TILES[j]
        T = 128 * len(tiles)
        hn2 = hn2TR[j % 2]
        hnk = ["hn2T%d_%d" % (j % 2, li) for li in range(len(tiles))]
        for s8 in range(8):
            slot = w1_cnt[0] % 3; w1_cnt[0] += 1
            if not (j == 0 and s8 < 3 and w1_pre[0]):
                dma("sp", W1R[slot], w1s.rearrange("(k p) n -> p k n", p=128)[:, :, s8 * 512:(s8 + 1) * 512], W1S_KEYS, ["W1R%d" % slot])
            for fl in range(4):
                fc = s8 * 4 + fl
                bk = fc % 2
                chunks = [(0, min(T, 512))] + ([(512, T)] if T > 512 else [])
                for (c0, c1) in chunks:
                    bank = (0 if bk == 0 else 2) + (0 if c0 == 0 else 1)
                    for k in range(8):
                        mm(ps[:, bank, 0:c1 - c0], W1R[slot][:, k, fl * 128:(fl + 1) * 128], hn2[:, k, c0:c1], k == 0, k == 7,
                           ["W1R%d" % slot] + hnk, ["ps%d" % bank])
                b0 = 0 if bk == 0 else 2
                act(rl[bk][:, 0:T], psf[:, b0 * 512:b0 * 512 + T], AF.Relu, ["ps%d" % b0, "ps%d" % (b0 + 1)], ["rl%d" % bk])
                tt(hT[:, fc, 0:T], rl[bk][:, 0:T], rl[bk][:, 0:T], ALU.mult, ["rl%d" % bk], ["hT_%d" % fc])

    def mlp_ff2(j):
        tiles = ST_TILES[j]
        hTk = ["hT_%d" % fc for fc in range(32)]
        g2k = {True: ["g2S0", "g2S1"], False: ["g2P0", "g2P1"]}
        for half in range(2):
            banks = [2, 3, 4, 5, 6] if half == 0 else [0, 1, 2, 3, 4]
            for s8 in range(8):
                slot = w2_cnt[0] % NW2; w2_cnt[0] += 1
                dma("sp", W2R[slot], w2s.rearrange("(c p) n -> p c n", p=128)[:, s8 * 4:(s8 + 1) * 4, half * 512:(half + 1) * 512],
                    W2S_KEYS, ["W2R%d" % slot])
                for li, i in enumerate(tiles):
                    bank = banks[li]
                    for fl in range(4):
                        fc = s8 * 4 + fl
                        mm(ps[:, bank, :], hT[:, fc, li * 128:(li + 1) * 128], W2R[slot][:, fl, :], fc == 0, fc == 31,
                           ["W2R%d" % slot, "hT_%d" % fc], ["ps%d" % bank])
            for li, i in enumerate(tiles):
                SMP = (i == 16)
                bank = banks[li]
                g2b = g2S if SMP else g2P
                hv = h[:, i, half * 512:(half + 1) * 512]
                ytmp = ytmpR[li]; yk = "ytmp%d" % li
                tt(ytmp, ps[:, bank, :], g2b[:, half * 512:(half + 1) * 512], ALU.mult, ["ps%d" % bank] + g2k[SMP], [yk])
                tt(hv, hv, ytmp, ALU.add, ["h%d" % i, yk], ["h%d" % i])
        for li, i in enumerate(tiles):
            dst = yp[i * 128:(i + 1) * 128, :] if i < 16 else ys
            dma("pool", dst, h[:, i, :], ["h%d" % i], ["y%d" % i], dsem="d_y%d" % (i % 4))

    NST = len(ST_TILES)
    EARLY_KEYS = ["t2"] + ["hn2T0_%d" % li for li in range(4)] + ["W1R%d" % q for q in range(3)]
    P.add("dve", lambda e: e.memset(smallf[:, 204:205], 0.0), reads=[], writes=WIN_KEYS + EARLY_KEYS)
    mlp_norm(0)
    for s8 in range(3):
        dma("sp", W1R[s8], w1s.rearrange("(k p) n -> p k n", p=128)[:, :, s8 * 512:(s8 + 1) * 512], W1S_KEYS, ["W1R%d" % s8])
    w1_pre[0] = True
    P.fence("dve", lambda e: e.memset(smallf[:, 202:203], 0.0))

    for j in range(NST):
        mlp_ff1(j)
        f2 = P.capture(mlp_ff2, j)
        if j + 1 < NST:
            nn = P.capture(mlp_norm, j + 1)
            P.flush_positions([(f2, 0.0, 1.0), (nn, 0.02, 0.6)])
        else:
            P.flush_positions([(f2, 0.0, 1.0)])

    P.emit(nc)
    return nc


_CACHE = {}


def kernel(x_prompt, x_sample, state_gla, cache_swa_k, cache_swa_v, c_prompt, c_sample,
           w_ada, b_ada, norm1_w, norm2_w, w_in, w_gate_up, b_gate, gla_norm_w,
           q_norm_w, k_norm_w, sinks, w_out, w_ff1, w_ff2):
    f = lambda a: np.ascontiguousarray(np.asarray(a, dtype=np.float32))
    if "nc" not in _CACHE:
        _CACHE["nc"] = build()
        _CACHE["consts"] = _consts()
    nc = _CACHE["nc"]
    cf, cb = _CACHE["consts"]
    x_prompt = f(x_prompt); x_sample = f(x_sample)
    in_maps = []
    for c in range(8):
        bs = slice(16 * c, 16 * c + 16)
        in_maps.append({
            "xp": x_prompt[c], "xs": x_sample[bs].reshape(128, 1024),
            "cvec": np.concatenate([f(c_prompt)[c:c + 1], f(c_sample)[bs]], axis=0),
            "st_in": f(state_gla)[0, bs], "ck_in": f(cache_swa_k)[0, bs].reshape(16, 128, 128),
            "cv_in": f(cache_swa_v)[0, bs].reshape(16, 128, 128),
            "w_ada": f(w_ada)[0], "b_ada": f(b_ada), "n1w": f(norm1_w)[0], "n2w": f(norm2_w)[0],
            "w_in": f(w_in)[0], "wgu": f(w_gate_up)[0], "b_gate": f(b_gate), "gnw": f(gla_norm_w)[0],
            "qnw": f(q_norm_w)[0], "knw": f(k_norm_w)[0], "sinks": f(sinks)[0], "w_out": f(w_out)[0],
            "w1": f(w_ff1)[0], "w2": f(w_ff2)[0], "cst_f": cf, "cst_b": cb,
        })
    if os.environ.get("KTRACE"):
        res = run_bass_kernel_spmd(nc, in_maps, core_ids=list(range(8)), trace=True)
        print("EXEC_TIME_NS", res.exec_time_ns)
    else:
        res = run_bass_kernel_spmd(nc, in_maps, core_ids=list(range(8)))
    R = res.results
    _CACHE["last"] = R
    yp = np.stack([R[c]["yp"] for c in range(8)], 0).reshape(8, 2048, 1024)
    ys = np.concatenate([R[c]["ys"].reshape(16, 8, 1024) for c in range(8)], 0)
    gp = np.stack([R[c]["gp"] for c in range(8)], 0)[None]
    kp = np.stack([R[c]["kp"].reshape(128, 2, 64) for c in range(8)], 0)[None]
    vp = np.stack([R[c]["vp"].reshape(128, 2, 64) for c in range(8)], 0)[None]
    gs = np.concatenate([R[c]["gs"] for c in range(8)], 0)[None]
    ks = np.concatenate([R[c]["ks"].reshape(16, 128, 2, 64) for c in range(8)], 0)[None]
    vs = np.concatenate([R[c]["vs"].reshape(16, 128, 2, 64) for c in range(8)], 0)[None]
    return tuple(np.ascontiguousarray(a.astype(np.float32)) for a in (yp, ys, gp, kp, vp, gs, ks, vs))
```
